# Optimizing a Trainium2 kernel written in Bass

```python
import jax, jax.numpy as jnp
from jax import lax
import numpy as np

D_MODEL = 1024
BATCH = 4
SEQ = 8192
DEPTH = 4

N_META = 16
N_HEADS = 16
HEAD_DIM = D_MODEL // N_HEADS
D_FF = 4 * D_MODEL
CONV_WIDTH = 3
BLOCK_Q = 128
N_A_LAYERS = DEPTH // 2
N_B_LAYERS = DEPTH - N_A_LAYERS
N_NORMS = 4
RMS_EPS = 1e-6
NORM_NOISE = 0.02

kernel_name = "yoco_shortconv_stickbreaking_hybrid"


def rms_norm(x, g):
    xf = x.astype(jnp.float32)
    y = xf * lax.rsqrt(jnp.mean(xf * xf, axis=-1, keepdims=True) + RMS_EPS)
    return (y * g.astype(jnp.float32)).astype(x.dtype)


def short_conv_mixer(h, w_in, conv_w, w_out):
    L = h.shape[1]
    b_gate, c_gate, xv = jnp.split(h @ w_in, 3, axis=-1)
    u = c_gate * xv
    u_pad = jnp.pad(u, ((0, 0), (CONV_WIDTH - 1, 0), (0, 0)))
    conv = u_pad[:, 0:L] * conv_w[0]
    for tap in range(1, CONV_WIDTH):
        conv = conv + u_pad[:, tap:tap + L] * conv_w[tap]
    return (b_gate * conv) @ w_out


def squared_relu_mlp(h, w1, w2):
    return jnp.square(jax.nn.relu(h @ w1)) @ w2


def stick_breaking_attention(q, k, v):
    L = q.shape[2]
    pad = (-N_META) % BLOCK_Q
    widths = ((0, 0), (0, 0), (pad, 0), (0, 0))
    qp, kp, vp = jnp.pad(q, widths), jnp.pad(k, widths), jnp.pad(v, widths)
    n_blocks = (L + pad) // BLOCK_Q
    scale = HEAD_DIM ** -0.5
    outs = []
    for i in range(n_blocks):
        q_lo = i * BLOCK_Q
        kv_hi = q_lo + BLOCK_Q
        qb = qp[:, :, q_lo:kv_hi]
        kb = kp[:, :, :kv_hi]
        vb = vp[:, :, :kv_hi]
        z = jnp.einsum('bhqd,bhkd->bhqk', qb, kb).astype(jnp.float32) * scale
        q_pos = q_lo + jnp.arange(BLOCK_Q)[:, None]
        k_pos = jnp.arange(kv_hi)[None, :]
        visible = (k_pos < q_pos) & (k_pos >= pad)
        log_beta = jax.nn.log_sigmoid(z)
        log_one_minus = jnp.where(visible, jax.nn.log_sigmoid(-z), 0.0)
        rev = lax.cumsum(log_one_minus, axis=3, reverse=True)
        after = jnp.pad(rev[..., 1:], ((0, 0), (0, 0), (0, 0), (0, 1)))
        a = jnp.where(visible, jnp.exp(log_beta + after), 0.0)
        outs.append(jnp.einsum('bhqk,bhkd->bhqd', a.astype(vb.dtype), vb))
    return jnp.concatenate(outs, axis=2)[:, :, pad:]


def setup_inputs(seed: int = 0) -> dict:
    key = jax.random.key(seed)
    ks = jax.random.split(key, 14)
    nA, nB = N_A_LAYERS, N_B_LAYERS
    f32 = jnp.float32
    dm = D_MODEL ** -0.5
    return {
        "x": jax.random.normal(ks[0], (BATCH, SEQ, D_MODEL), f32),
        "meta_tokens": jax.random.normal(ks[1], (N_META, D_MODEL), f32),
        "norm_gains": 1.0 + NORM_NOISE * jax.random.normal(ks[2], (DEPTH, N_NORMS, D_MODEL), f32),
        "conv_in_proj": jax.random.normal(ks[3], (nA, D_MODEL, 3 * D_MODEL), f32) * dm,
        "conv_w": jax.random.normal(ks[4], (nA, CONV_WIDTH, D_MODEL), f32) * CONV_WIDTH ** -0.5,
        "conv_out_proj": jax.random.normal(ks[5], (nA, D_MODEL, D_MODEL), f32) * dm,
        "kv_norm": 1.0 + NORM_NOISE * jax.random.normal(ks[6], (D_MODEL,), f32),
        "w_k": jax.random.normal(ks[7], (D_MODEL, D_MODEL), f32) * dm,
        "w_v": jax.random.normal(ks[8], (D_MODEL, D_MODEL), f32) * dm,
        "w_q": jax.random.normal(ks[9], (nB, D_MODEL, D_MODEL), f32) * dm,
        "w_o": jax.random.normal(ks[10], (nB, D_MODEL, D_MODEL), f32) * dm,
        "mlp_w1": jax.random.normal(ks[11], (DEPTH, D_MODEL, D_FF), f32) * dm,
        "mlp_w2": jax.random.normal(ks[12], (DEPTH, D_FF, D_MODEL), f32) * D_FF ** -0.5,
    }


def reference(x, meta_tokens, norm_gains, conv_in_proj, conv_w, conv_out_proj,
              kv_norm, w_k, w_v, w_q, w_o, mlp_w1, mlp_w2):
    bsz = x.shape[0]
    meta = jnp.broadcast_to(meta_tokens[None].astype(x.dtype), (bsz, N_META, D_MODEL))
    h = jnp.concatenate([meta, x], axis=1)
    L = h.shape[1]

    def split_heads(t):
        return t.reshape(bsz, L, N_HEADS, HEAD_DIM).transpose(0, 2, 1, 3)

    k_shared = v_shared = None
    for layer in range(DEPTH):
        g = norm_gains[layer]
        if layer < N_A_LAYERS:
            mix = short_conv_mixer(rms_norm(h, g[0]), conv_in_proj[layer], conv_w[layer],
                                   conv_out_proj[layer])
        else:
            if layer == N_A_LAYERS:
                hk = rms_norm(h, kv_norm)
                k_shared = split_heads(hk @ w_k)
                v_shared = split_heads(hk @ w_v)
            j = layer - N_A_LAYERS
            q = split_heads(rms_norm(h, g[0]) @ w_q[j])
            o = stick_breaking_attention(q, k_shared, v_shared)
            mix = o.transpose(0, 2, 1, 3).reshape(bsz, L, D_MODEL) @ w_o[j]
        h = h + rms_norm(mix, g[1])
        ff = squared_relu_mlp(rms_norm(h, g[2]), mlp_w1[layer], mlp_w2[layer])
        h = h + rms_norm(ff, g[3])
    return h[:, N_META:]
```

```python
import numpy as np
import ml_dtypes
from contextlib import ExitStack
import concourse.bass as bass
import concourse.mybir as mybir
from concourse.bass_utils import run_bass_kernel_spmd

F32 = mybir.dt.float32
BF16 = mybir.dt.bfloat16
AF = mybir.ActivationFunctionType
ALU = mybir.AluOpType

D = 1024
NMETA = 16
NSLOT = 4
ENGS = ["pe", "act", "dve", "pool", "sp"]


class _Sem:
    def __init__(self, h, i):
        self.h = h
        self.i = i


class Sched:
    def __init__(self, new_sem):
        self.new_sem = new_sem
        self.ops = {e: [] for e in ENGS}
        self.cnt = {e: 0 for e in ENGS}
        self.prog = {e: new_sem("prog_" + e) for e in ENGS}
        self.waited = {e: {} for e in ENGS}
        self.lastw = {}
        self.readers = {}
        self.pools = {}
        self.n_wait = 0

    def add_pool(self, name, n):
        self.pools[name] = {"sems": [[self.new_sem("%s%d" % (name, i)), 0] for i in range(n)], "next": 0}

    def _need(self, E, tok):
        sem, val, eng = tok
        if eng == E and E == "pe":
            return
        w = self.waited[E]
        if w.get(sem.i, 0) >= val:
            return
        w[sem.i] = val
        h = sem.h
        self.n_wait += 1
        self.ops[E].append(lambda e, h=h, val=val: e.wait_ge(h, val))

    def _deps(self, E, reads, writes):
        for k in reads:
            t = self.lastw.get(k)
            if t is not None:
                self._need(E, t)
        for k in writes:
            t = self.lastw.get(k)
            if t is not None:
                self._need(E, t)
            r = self.readers.get(k)
            if r:
                for t in r.values():
                    self._need(E, t)

    def _commit(self, tok, reads, writes):
        for k in writes:
            self.lastw[k] = tok
            self.readers[k] = {}
        for k in reads:
            d = self.readers.setdefault(k, {})
            old = d.get(tok[0].i)
            if old is None or old[1] < tok[1]:
                d[tok[0].i] = tok

    def op(self, E, fn, reads=(), writes=()):
        self._deps(E, reads, writes)
        self.cnt[E] += 1
        sem = self.prog[E]
        h = sem.h
        self.ops[E].append(lambda e, fn=fn, h=h: fn(e).then_inc(h, 1))
        tok = (sem, self.cnt[E], E)
        self._commit(tok, reads, writes)
        return tok

    def dma(self, Q, pool, out_ap, in_ap, reads=(), writes=()):
        self._deps(Q, reads, writes)
        p = self.pools[pool]
        ent = p["sems"][p["next"] % len(p["sems"])]
        p["next"] += 1
        sem = ent[0]
        if ent[1] > 0:
            self._need(Q, (sem, 16 * ent[1], None))
        ent[1] += 1
        h = sem.h
        self.ops[Q].append(lambda e, o=out_ap, i=in_ap, h=h: e.dma_start(out=o, in_=i).then_inc(h, 16))
        tok = (sem, 16 * ent[1], None)
        self._commit(tok, reads, writes)
        return tok

    def barrier(self, dummy):
        M = "pool"
        for E in ENGS:
            if E != M and self.cnt[E] > 0:
                self._need(M, (self.prog[E], self.cnt[E], E))
        for p in self.pools.values():
            for ent in p["sems"]:
                if ent[1] > 0:
                    self._need(M, (ent[0], 16 * ent[1], None))
        tok = self.op(M, lambda e: e.memset(dummy, 0.0))
        for E in ENGS:
            if E != M:
                self._need(E, tok)
        self.lastw.clear()
        self.readers.clear()


def build(nsg):
    nc = bass.Bass("TRN2", target_bir_lowering=False)
    TSEQ = nsg * 1024
    NKB = 8 * nsg + 1
    NK = NKB * 128

    def din(name, shape, dt=F32):
        return nc.dram_tensor(name, list(shape), dt, kind="ExternalInput").ap()

    xT = din("xT", [D, TSEQ])
    metaT = din("metaT", [D, NMETA])
    gains_d = din("gains", [128, 17 * 8])
    convw_d = din("convw", [128, 48])
    pid_d = din("pid", [128, 2])
    dmask_d = din("dmask", [128, 8 * 512], BF16)
    negT_d = din("negT", [128, 128], BF16)
    negones_d = din("negones", [128, 128], BF16)
    ones_d = din("ones", [128, 128], BF16)
    ident_d = din("ident", [128, 128], BF16)
    cin = din("conv_in_proj", [2, D, 3 * D])
    cout = din("conv_out_proj", [2, D, D])
    wk = din("w_k", [D, D])
    wv = din("w_v", [D, D])
    wq = din("w_q", [2, D, D])
    wo = din("w_o", [2, D, D])
    w1 = din("mlp_w1", [4, D, 4 * D])
    w2 = din("mlp_w2", [4, 4 * D, D])
    outT = nc.dram_tensor("outT", [D, nsg * 512], F32, kind="ExternalOutput").ap()

    wsl = nc.dram_tensor("wsl", [92, 128, 4096], BF16).ap()
    kTd = nc.dram_tensor("kTd", [8, 128, NK], BF16).ap()
    vd = nc.dram_tensor("vd", [8, 128, NKB, 128], BF16).ap()

    def k1024(W, j):
        return W.rearrange("(kc p) n -> p kc n", p=128)[:, :, j * 512:(j + 1) * 512]

    def w2slab(W, hh, j):
        return W.rearrange("(hh kc p) n -> hh p kc n", hh=2, p=128)[hh][:, :, j * 256:(j + 1) * 256]

    slab_src = []

    def add_mlp(l):
        for hh in range(2):
            for j in range(4):
                slab_src.append((k1024(w1[l], hh * 4 + j), 8))
            for j in range(4):
                slab_src.append((w2slab(w2[l], hh, j), 16))

    INPERM = [0, 1, 2, 4, 3, 5]
    for l in range(2):
        for j in range(6):
            slab_src.append((k1024(cin[l], INPERM[j]), 8))
        for j in range(2):
            slab_src.append((k1024(cout[l], j), 8))
        add_mlp(l)
    for j in range(2):
        slab_src.append((k1024(wk, j), 8))
    for j in range(2):
        slab_src.append((k1024(wv, j), 8))
    for lb in range(2):
        for j in range(2):
            slab_src.append((k1024(wq[lb], j), 8))
        for j in range(2):
            slab_src.append((k1024(wo[lb], j), 8))
        add_mlp(2 + lb)
    assert len(slab_src) == 92

    slab_seq = list(range(52))
    for S in range(nsg):
        slab_seq += list(range(92))

    with ExitStack() as es:
        def sb(name, shape, dt):
            return es.enter_context(nc.sbuf_tensor(name, list(shape), dt))

        sem_count = [0]

        def new_sem(name):
            h = es.enter_context(nc.semaphore(name))
            sem_count[0] += 1
            return _Sem(h, sem_count[0])

        slabs = [sb("slab%d" % i, [128, 4096], BF16) for i in range(NSLOT)]
        R1 = sb("R1", [128, 8192], F32)
        R2 = sb("R2", [128, 8192], F32)
        hn_t = sb("hn", [128, 8192], BF16)
        sqt = [sb("sq%d" % i, [128, 1024], BF16) for i in range(2)]
        lnv = sb("lnv", [128, 1024], F32)
        rstd = sb("rstd", [128, 1024], F32)
        tmpc = [sb("tmpc%d" % i, [128, 1024], F32) for i in range(2)]
        dmask_sb = sb("dmask_sb", [128, 4096], BF16)
        gains_sb = sb("gains_sb", [128, 17 * 8], F32)
        convw_sb = sb("convw_sb", [128, 48], F32)
        pid_sb = sb("pid_sb", [128, 2], F32)
        negT_sb = sb("negT_sb", [128, 128], BF16)
        negones_sb = sb("negones_sb", [128, 128], BF16)
        ones_sb = sb("ones_sb", [128, 128], BF16)
        ident_sb = sb("ident_sb", [128, 128], BF16)
        ztile = sb("ztile", [128, 128], BF16)
        cst = sb("cst", [128, 4], F32)
        ust = sb("ust", [128, 32], F32)
        dummy = sb("dummyt", [128, 4], F32)
        arena = sb("arena", [128, 16384], F32)
        ps = es.enter_context(nc.psum_tensor("ps", [128, 8, 512], F32))

        sc = Sched(new_sem)
        for i in range(NSLOT):
            sc.add_pool("slab%d" % i, 1)
        sc.add_pool("x", 2)
        sc.add_pool("const", 4)
        sc.add_pool("pl", 3)
        sc.add_pool("pst", 2)
        sc.add_pool("kvst", 8)
        sc.add_pool("kvld", 4)
        sc.add_pool("out", 2)

        def carve(off, n, dt):
            if dt == F32:
                return arena[:, off // 4: off // 4 + n]
            return arena[:, off // 4: off // 4 + n // 2].bitcast(BF16)

        h = R1[:, :].rearrange("p (c t) -> p c t", c=8)
        mix = R2[:, :].rearrange("p (c t) -> p c t", c=8)
        hn = hn_t[:, :].rearrange("p (c t) -> p c t", c=8)
        hown = R2[:, 0:4096].rearrange("p (c t) -> p c t", c=8)
        mixB = R2[:, 4096:8192].rearrange("p (c t) -> p c t", c=8)
        hnB = hn_t[:, 0:4096].rearrange("p (c t) -> p c t", c=8)
        qT = R1[:, 0:2048].bitcast(BF16).rearrange("p (c t) -> p c t", c=8)
        oT = R1[:, 2048:4096].bitcast(BF16).rearrange("p (c t) -> p c t", c=8)
        slab8 = [s[:, :].rearrange("p (k o) -> p k o", k=8) for s in slabs]
        slab16 = [s[:, :].rearrange("p (k o) -> p k o", k=16) for s in slabs]
        dmask3 = dmask_sb[:, :].rearrange("p (r t) -> p r t", r=8)
        ust3l = [ust[:, l * 16:(l + 1) * 16].rearrange("p (c t) -> p c t", c=8) for l in range(2)]
        eps_ap = cst[:, 0:1]
        one_ap = cst[:, 1:2]

        bgate = carve(0, 8192, BF16).rearrange("p (c t) -> p c t", c=8)
        cy = carve(16384, 8192, BF16).rearrange("p (c t) -> p c t", c=8)
        ub = [carve(32768 + i * 4112, 1028, F32) for i in range(2)]
        ctmp = [carve(41216 + i * 4096, 1024, F32) for i in range(2)]
        hid = carve(0, 16384, BF16).rearrange("p (c t) -> p c t", c=16)
        hidB = carve(0, 8192, BF16).rearrange("p (c t) -> p c t", c=16)
        rtmp = [carve(32768 + i * 2048, 512, F32) for i in range(2)]
        ktmp = [carve(i * 1024, 512, BF16) for i in range(2)]
        vtmp = [carve(2048 + i * 1024, 512, BF16) for i in range(2)]
        kTs = carve(0, 8320, BF16)
        vs = carve(16640, 8320, BF16).rearrange("p (k f) -> p k f", f=128)
        EA = [carve(off_, 2048, BF16).rearrange("p (a t) -> p a t", a=4) for off_ in (33280, 37376, 45568)]
        sp2 = [carve(41472 + i * 2048, 1024, BF16).rearrange("p (a t) -> p a t", a=2) for i in range(2)]
        Lsum = carve(49664, 1024, BF16).rearrange("p (a t) -> p a t", a=2)
        stg = [carve(i * 16384, 4096, F32) for i in range(3)]
        cb = [carve(49152 + i * 8192, 4096, BF16) for i in range(2)]

        st = {"acc": 0, "alt": 0, "sq": 0, "slab_k": 0, "slab_loaded": 0, "rt": 0, "kt": 0, "vt": 0, "tc": 0}

        def acc():
            st["acc"] = (st["acc"] + 1) % 4
            return st["acc"]

        def alt():
            st["alt"] ^= 1
            return "dve" if st["alt"] else "pool"

        def barrier():
            sc.barrier(dummy[:, 0:1])

        for dst, src, key in [(gains_sb, gains_d, "gains"), (convw_sb, convw_d, "convw"), (pid_sb, pid_d, "pid"),
                              (dmask_sb, dmask_d, "dmask"), (negT_sb, negT_d, "negT"),
                              (negones_sb, negones_d, "negones"), (ones_sb, ones_d, "ones"),
                              (ident_sb, ident_d, "ident")]:
            sc.dma("sp", "const", dst[:, :], src, writes=[key])
        sc.op("pool", lambda e: e.memset(ztile[:, :], 0.0), writes=["ztile"])
        sc.op("pool", lambda e: e.memset(cst[:, 0:1], 1e-6), writes=["cst0"])
        sc.op("pool", lambda e: e.memset(cst[:, 1:2], 1.0), writes=["cst1"])
        sc.op("pool", lambda e: e.memset(ust[:, :], 0.0), writes=[("ust", l_, f) for l_ in range(2) for f in range(8)])
        kT_keys = [[] for _ in range(8)]
        v_keys = [[] for _ in range(8)]
        for c in range(8):
            sc.dma("pool", "kvst", kTd[c, :, 0:112], ztile[:, 0:112], reads=["ztile"], writes=[("kTd", c, "z")])
            kT_keys[c].append(("kTd", c, "z"))
            sc.dma("pool", "kvst", vd[c, 0:112, 0, :], ztile[0:112, :], reads=["ztile"], writes=[("vd", c, "z")])
            v_keys[c].append(("vd", c, "z"))

        fold = {}
        for l_ in range(4):
            base_ = (24 * l_ + 8) if l_ < 2 else (52 + 20 * (l_ - 2) + 4)
            for hh_ in range(2):
                for j_ in range(4):
                    fold[base_ + hh_ * 8 + j_] = l_ * 4 + 2
        for lb_ in range(2):
            for j_ in range(2):
                fold[52 + 20 * lb_ + j_] = (2 + lb_) * 4 + 0
        for s, (src, kc) in enumerate(slab_src):
            i3 = s % 3
            i2 = s % 2
            dst = stg[i3].rearrange("p (k o) -> p k o", k=kc)
            sc.dma("sp" if s % 2 == 0 else "act", "pl", dst, src, writes=[("stg", i3)])
            eng = "dve" if s % 2 == 0 else "pool"
            if s in fold:
                gi = fold[s]
                for kq in range(8):
                    sc.op("dve", lambda e, o=cb[i2][:, kq * 512:(kq + 1) * 512], i=stg[i3][:, kq * 512:(kq + 1) * 512],
                          g=gains_sb[:, gi * 8 + kq:gi * 8 + kq + 1]:
                          e.tensor_scalar(out=o, in0=i, scalar1=g, scalar2=None, op0=ALU.mult),
                          reads=[("stg", i3), "gains"], writes=[("cb", i2)])
            else:
                sc.op(eng, lambda e, o=cb[i2], i=stg[i3]: e.tensor_copy(out=o, in_=i),
                      reads=[("stg", i3)], writes=[("cb", i2)])
            sc.dma("pool", "pst", wsl[s], cb[i2], reads=[("cb", i2)], writes=[("wsl", s)])
        barrier()

        def issue_slab_loads(upto):
            while st["slab_loaded"] <= upto and st["slab_loaded"] < len(slab_seq):
                k = st["slab_loaded"]
                slot = k % NSLOT
                sc.dma("sp", "slab%d" % slot, slabs[slot][:, :], wsl[slab_seq[k]], writes=[("slab", slot)])
                st["slab_loaded"] += 1

        def next_slab(expect, hold=0):
            k = st["slab_k"]
            assert slab_seq[k] == expect, (k, slab_seq[k], expect)
            issue_slab_loads(k + NSLOT - 1 - hold)
            st["slab_k"] += 1
            return k % NSLOT

        def mm(out, lhsT, rhs, start, stop, reads, writes):
            sc.op("pe", lambda e: e.matmul(out, lhsT=lhsT, rhs=rhs, start=start, stop=stop),
                  reads=reads, writes=writes)

        def norm_stats(src, skeys, T, nts):
            for ni, (n0, N) in enumerate(nts):
                for c in range(8):
                    j = st["sq"]
                    st["sq"] ^= 1
                    sc.op("act", lambda e, o=sqt[j][:, :N], i=src[:, c, n0:n0 + N]:
                          e.activation(out=o, in_=i, func=AF.Square),
                          reads=skeys(c, ni), writes=[("sq", j)])
                    mm(ps[:, 4 + ni, :N], ones_sb[:, :], sqt[j][:, :N], c == 0, c == 7,
                       reads=[("sq", j), "ones"], writes=[("ps", 4 + ni)])
                sc.op("act", lambda e, o=lnv[:, n0:n0 + N], i=ps[:, 4 + ni, :N]:
                      e.activation(out=o, in_=i, func=AF.Ln, bias=eps_ap, scale=1.0 / D),
                      reads=[("ps", 4 + ni), "cst0"], writes=[("lnv", ni)])
                sc.op("act", lambda e, o=rstd[:, n0:n0 + N], i=lnv[:, n0:n0 + N]:
                      e.activation(out=o, in_=i, func=AF.Exp, scale=-0.5),
                      reads=[("lnv", ni)], writes=[("rstd", ni)])

        def apply_pre(gi, src, skeys, dst, dkey, T, nts):
            for ni, (n0, N) in enumerate(nts):
                for c in range(8):
                    g = gains_sb[:, gi * 8 + c: gi * 8 + c + 1]
                    sc.op("dve", lambda e, o=dst[:, c, n0:n0 + N], i=src[:, c, n0:n0 + N], g=g,
                          r=rstd[:, n0:n0 + N]:
                          e.scalar_tensor_tensor(out=o, in0=i, scalar=g, in1=r, op0=ALU.mult, op1=ALU.mult),
                          reads=skeys(c, ni) + [("rstd", ni), "gains"], writes=[(dkey, c, ni)])

        def apply_post(gi, src, skeys, res, rkeys, T, nts):
            for ni, (n0, N) in enumerate(nts):
                for c in range(8):
                    g = gains_sb[:, gi * 8 + c: gi * 8 + c + 1]
                    j = st["tc"]
                    st["tc"] ^= 1
                    sc.op("dve", lambda e, o=tmpc[j][:, :N], i=src[:, c, n0:n0 + N], g=g, r=rstd[:, n0:n0 + N]:
                          e.scalar_tensor_tensor(out=o, in0=i, scalar=g, in1=r, op0=ALU.mult, op1=ALU.mult),
                          reads=skeys(c, ni) + [("rstd", ni), "gains"], writes=[("tc", j)])
                    sc.op("pool" if c % 2 == 0 else "dve", lambda e, o=res[:, c, n0:n0 + N], i=tmpc[j][:, :N]:
                          e.tensor_tensor(out=o, in0=o, in1=i, op=ALU.add),
                          reads=[("tc", j)] + rkeys(c, ni), writes=rkeys(c, ni))

        def cast_pre(src, skeys, dst, dkey, T, nts):
            for ni, (n0, N) in enumerate(nts):
                for c in range(8):
                    sc.op("act", lambda e, o=dst[:, c, n0:n0 + N], i=src[:, c, n0:n0 + N]:
                          e.activation(out=o, in_=i, func=AF.Copy),
                          reads=skeys(c, ni), writes=[(dkey, c, ni)])
            norm_stats(src, skeys, T, nts)

        def order(first, nts):
            if first and len(nts) > 1:
                return [(ocl, ni) for ni in range(len(nts)) for ocl in range(4)]
            return [(ocl, ni) for ocl in range(4) for ni in range(len(nts))]

        def slab_plan(nslabs, nts):
            plan = []
            j0 = 0
            if len(nts) > 1 and nslabs >= 2:
                for ni in range(len(nts)):
                    for jj in range(2):
                        for ocl in range(4):
                            plan.append((jj, ocl, ni))
                j0 = 2
            for jj in range(j0, nslabs):
                for ocl in range(4):
                    for ni in range(len(nts)):
                        plan.append((jj, ocl, ni))
            return plan

        pend = []

        def _emit_stat(item):
            j, oc, ni, N = item
            mm(ps[:, 4 + ni, :N], ones_sb[:, :], sqt[j][:, :N], oc == 0, oc == 7,
               reads=[("sq", j), "ones"], writes=[("ps", 4 + ni)])

        def post_stats_add(src_ap, src_keys, oc, ni, N):
            j = st["sq"]
            st["sq"] ^= 1
            sc.op("act", lambda e, o=sqt[j][:, :N], i=src_ap: e.activation(out=o, in_=i, func=AF.Square),
                  reads=src_keys, writes=[("sq", j)])
            pend.append((j, oc, ni, N))
            while len(pend) > 1:
                _emit_stat(pend.pop(0))

        def post_stats_finish(nts):
            while pend:
                _emit_stat(pend.pop(0))
            for ni, (n0, N) in enumerate(nts):
                sc.op("act", lambda e, o=lnv[:, n0:n0 + N], i=ps[:, 4 + ni, :N]:
                      e.activation(out=o, in_=i, func=AF.Ln, bias=eps_ap, scale=1.0 / D),
                      reads=[("ps", 4 + ni), "cst0"], writes=[("lnv", ni)])
                sc.op("act", lambda e, o=rstd[:, n0:n0 + N], i=lnv[:, n0:n0 + N]:
                      e.activation(out=o, in_=i, func=AF.Exp, scale=-0.5),
                      reads=[("lnv", ni)], writes=[("rstd", ni)])

        def mlp(l, sbase, res, rkeys, mx, mkey, hnv, hidv, T, nts):
            barrier()
            cast_pre(res, rkeys, hnv, "hn", T, nts)
            for ni, (n0, N) in enumerate(nts):
                sc.op("dve", lambda e, o=lnv[:, n0:n0 + N], i=rstd[:, n0:n0 + N]:
                      e.tensor_tensor(out=o, in0=i, in1=i, op=ALU.mult),
                      reads=[("rstd", ni), ("lnv", ni)], writes=[("lnv", ni)])
            si = sbase
            for hh in range(2):
                slots = {}
                for jj, ocl, ni in slab_plan(4, nts) if hh == 0 else [(jj, ocl, ni) for jj in range(4)
                                                                      for ocl in range(4) for ni in range(len(nts))]:
                    if jj not in slots:
                        for k_ in range(len(slots), jj + 1):
                            slots[k_] = next_slab(si, hold=1 if (k_ == 1 and len(nts) > 1) else 0)
                            si += 1
                    slot = slots[jj]
                    sl = slab8[slot]
                    n0, N = nts[ni]
                    hc = jj * 4 + ocl
                    b = acc()
                    for kc in range(8):
                        mm(ps[:, b, :N], sl[:, kc, ocl * 128:(ocl + 1) * 128], hnv[:, kc, n0:n0 + N],
                           kc == 0, kc == 7, reads=[("slab", slot), ("hn", kc, ni)], writes=[("ps", b)])
                    ri = st["rt"]
                    st["rt"] ^= 1
                    sc.op("act", lambda e, o=rtmp[ri][:, :N], i=ps[:, b, :N]:
                          e.activation(out=o, in_=i, func=AF.Relu),
                          reads=[("ps", b)], writes=[("rt", ri)])
                    sc.op("act", lambda e, o=rtmp[ri][:, :N]: e.activation(out=o, in_=o, func=AF.Square),
                          reads=[("rt", ri)], writes=[("rt", ri)])
                    sc.op(alt(), lambda e, o=hidv[:, hc, n0:n0 + N], i=rtmp[ri][:, :N], r2=lnv[:, n0:n0 + N]:
                          e.tensor_tensor(out=o, in0=i, in1=r2, op=ALU.mult),
                          reads=[("rt", ri), ("lnv", ni)], writes=[("hid", hc, ni)])
                for j in range(4):
                    slot = next_slab(si)
                    si += 1
                    sl = slab16[slot]
                    for ocl in range(2):
                        oc = j * 2 + ocl
                        for ni, (n0, N) in enumerate(nts):
                            b = acc()
                            for kc in range(16):
                                mm(ps[:, b, :N], sl[:, kc, ocl * 128:(ocl + 1) * 128], hidv[:, kc, n0:n0 + N],
                                   kc == 0, kc == 15, reads=[("slab", slot), ("hid", kc, ni)], writes=[("ps", b)])
                            if hh == 0:
                                sc.op("act", lambda e, o=mx[:, oc, n0:n0 + N], i=ps[:, b, :N]:
                                      e.activation(out=o, in_=i, func=AF.Copy),
                                      reads=[("ps", b)], writes=[(mkey, oc, ni)])
                            else:
                                sc.op("dve", lambda e, o=mx[:, oc, n0:n0 + N], i=ps[:, b, :N]:
                                      e.tensor_tensor(out=o, in0=i, in1=o, op=ALU.add),
                                      reads=[("ps", b), (mkey, oc, ni)], writes=[(mkey, oc, ni)])
                                post_stats_add(mx[:, oc, n0:n0 + N], [(mkey, oc, ni)], oc, ni, N)
            mk = lambda c, ni: [(mkey, c, ni)]
            post_stats_finish(nts)
            apply_post(l * 4 + 3, mx, mk, res, rkeys, T, nts)

        def conv_layer(l, T, nts):
            nn = len(nts)
            sbase = 24 * l
            hk = lambda c, ni: [("h", c, ni)]
            barrier()
            norm_stats(h, hk, T, nts)
            apply_pre(l * 4 + 0, h, hk, hn, "hn", T, nts)
            si = sbase
            slots = {}
            done = {}
            for sj, ocl, ni in slab_plan(6, nts):
                if True:
                    if sj not in slots:
                        for k_ in range(len(slots), sj + 1):
                            slots[k_] = next_slab(si, hold=1 if (k_ == 1 and len(nts) > 1) else 0)
                            si += 1
                    slot = slots[sj]
                    sl = slab8[slot]
                    n0, N = nts[ni]
                    oc = INPERM[sj] * 4 + ocl
                    f = oc % 8
                    j = f % 2
                    b = acc()
                    for kc in range(8):
                        mm(ps[:, b, :N], sl[:, kc, ocl * 128:(ocl + 1) * 128], hn[:, kc, n0:n0 + N],
                           kc == 0, kc == 7, reads=[("slab", slot), ("hn", kc, ni)], writes=[("ps", b)])
                    if oc < 8:
                        sc.op("act", lambda e, o=bgate[:, f, n0:n0 + N], i=ps[:, b, :N]:
                              e.activation(out=o, in_=i, func=AF.Copy),
                              reads=[("ps", b)], writes=[("bg", f, ni)])
                    elif oc < 16:
                        sc.op("act", lambda e, o=cy[:, f, n0:n0 + N], i=ps[:, b, :N]:
                              e.activation(out=o, in_=i, func=AF.Copy),
                              reads=[("ps", b)], writes=[("cy", f, ni)])
                    else:
                        sc.op("dve", lambda e, o=ub[j][:, 2 + n0:2 + n0 + N], i=ps[:, b, :N],
                              c_=cy[:, f, n0:n0 + N]: e.tensor_tensor(out=o, in0=i, in1=c_, op=ALU.mult),
                              reads=[("ps", b), ("cy", f, ni)], writes=[("ub", j, ni)])
                    done[oc] = done.get(oc, 0) + 1
                    if oc >= 16 and done[oc] == nn:
                        ubk = [("ub", j, n_) for n_ in range(nn)]
                        sc.op("pool", lambda e, o=ub[j][:, 0:2], i=ust3l[l][:, f, :]: e.tensor_copy(out=o, in_=i),
                              reads=[("ust", l, f)], writes=[("ubs", j)])
                        cwi = lambda tap: convw_sb[:, (l * 3 + tap) * 8 + f:(l * 3 + tap) * 8 + f + 1]
                        sc.op("act", lambda e, o=ctmp[j][:, :T], i=ub[j][:, 0:T], s_=cwi(0):
                              e.activation(out=o, in_=i, func=AF.Copy, scale=s_),
                              reads=ubk + [("ubs", j), "convw"], writes=[("ct", j)])
                        for tap in (1, 2):
                            sc.op("dve", lambda e, o=ctmp[j][:, :T], i=ub[j][:, tap:tap + T], s_=cwi(tap):
                                  e.scalar_tensor_tensor(out=o, in0=i, scalar=s_, in1=o, op0=ALU.mult, op1=ALU.add),
                                  reads=ubk + [("ubs", j), ("ct", j), "convw"], writes=[("ct", j)])
                        sc.op("pool" if f % 2 == 0 else "dve",
                              lambda e, o=cy[:, f, :T], i=ctmp[j][:, :T], g=bgate[:, f, :T]:
                              e.tensor_tensor(out=o, in0=i, in1=g, op=ALU.mult),
                              reads=[("ct", j)] + [("bg", f, n_) for n_ in range(nn)],
                              writes=[("cy", f, n_) for n_ in range(nn)])
                        sc.op("pool", lambda e, o=ust3l[l][:, f, :], i=ub[j][:, T:T + 2]: e.tensor_copy(out=o, in_=i),
                              reads=ubk + [("ubs", j)], writes=[("ust", l, f)])
            for sj in range(2):
                slot = next_slab(si)
                si += 1
                sl = slab8[slot]
                for ocl in range(4):
                    oc = sj * 4 + ocl
                    for ni, (n0, N) in enumerate(nts):
                        b = acc()
                        for kc in range(8):
                            mm(ps[:, b, :N], sl[:, kc, ocl * 128:(ocl + 1) * 128], cy[:, kc, n0:n0 + N],
                               kc == 0, kc == 7, reads=[("slab", slot), ("cy", kc, ni)], writes=[("ps", b)])
                        sc.op("act", lambda e, o=mix[:, oc, n0:n0 + N], i=ps[:, b, :N]:
                              e.activation(out=o, in_=i, func=AF.Copy),
                              reads=[("ps", b)], writes=[("mix", oc, ni)])
                        post_stats_add(ps[:, b, :N], [("ps", b)], oc, ni, N)
            mk = lambda c, ni: [("mix", c, ni)]
            post_stats_finish(nts)
            apply_post(l * 4 + 1, mix, mk, h, hk, T, nts)
            mlp(l, sbase + 8, h, hk, mix, "mix", hn, hid, T, nts)

        def kv_phase(T, nts, kcol0, kb0, meta):
            hk = lambda c, ni: [("h", c, ni)]
            barrier()
            norm_stats(h, hk, T, nts)
            apply_pre(16, h, hk, hn, "hn", T, nts)
            si = 48
            for sj in range(2):
                slot = next_slab(si)
                si += 1
                sl = slab8[slot]
                for ocl, ni in order(sj == 0, nts):
                    n0, N = nts[ni]
                    oc = sj * 4 + ocl
                    b = acc()
                    for kc in range(8):
                        mm(ps[:, b, :N], sl[:, kc, ocl * 128:(ocl + 1) * 128], hn[:, kc, n0:n0 + N],
                           kc == 0, kc == 7, reads=[("slab", slot), ("hn", kc, ni)], writes=[("ps", b)])
                    ki = st["kt"]
                    st["kt"] ^= 1
                    sc.op("act" if ni == 0 else "dve",
                          (lambda e, o=ktmp[ki][:, :N], i=ps[:, b, :N]: e.activation(out=o, in_=i, func=AF.Copy))
                          if ni == 0 else
                          (lambda e, o=ktmp[ki][:, :N], i=ps[:, b, :N]: e.tensor_copy(out=o, in_=i)),
                          reads=[("ps", b)], writes=[("kt", ki)])
                    key = ("kTd", oc, kcol0 + n0)
                    sc.dma("pool", "kvst", kTd[oc, :, kcol0 + n0:kcol0 + n0 + N], ktmp[ki][:, :N],
                           reads=[("kt", ki)], writes=[key])
                    kT_keys[oc].append(key)
            for sj in range(2):
                slot = next_slab(si)
                si += 1
                sl = slab8[slot]
                nblk = 1 if meta else T // 128
                for jb in range(nblk):
                    M = NMETA if meta else 128
                    t0 = jb * 128
                    ni = t0 // 512
                    p0 = 112 if meta else 0
                    kb = kb0 + jb
                    b = acc()
                    for kc in range(8):
                        mm(ps[:M, b, :], hn[:, kc, t0:t0 + M], sl[:, kc, :], kc == 0, kc == 7,
                           reads=[("slab", slot), ("hn", kc, ni)], writes=[("ps", b)])
                    vi = st["vt"]
                    st["vt"] ^= 1
                    sc.op("act" if jb % 2 == 0 else "dve",
                          (lambda e, o=vtmp[vi][:M, :], i=ps[:M, b, :]: e.activation(out=o, in_=i, func=AF.Copy))
                          if jb % 2 == 0 else
                          (lambda e, o=vtmp[vi][:M, :], i=ps[:M, b, :]: e.tensor_copy(out=o, in_=i)),
                          reads=[("ps", b)], writes=[("vt", vi)])
                    dst = vd[sj * 4:(sj + 1) * 4, p0:p0 + M, kb, :].rearrange("c p f -> p c f")
                    srcv = vtmp[vi][:M, :].rearrange("p (c f) -> p c f", c=4)
                    keys = [("vd", sj * 4 + cc, kb) for cc in range(4)]
                    sc.dma("pool", "kvst", dst, srcv, reads=[("vt", vi)], writes=keys)
                    for cc in range(4):
                        v_keys[sj * 4 + cc].append(keys[cc])

        def kv_load(S, c, which):
            nkb = 8 * S + 9
            h0 = nkb // 2
            a_, bnd = (h0, nkb) if which == "hi" else (0, h0)
            sc.dma("pool", "kvld", kTs[:, a_ * 128:bnd * 128], kTd[c, :, a_ * 128:bnd * 128],
                   reads=list(kT_keys[c]), writes=[("kTs", which)])
            sc.dma("pool", "kvld", vs[:, a_:bnd, :], vd[c, :, a_:bnd, :],
                   reads=list(v_keys[c]), writes=[("vs", which)])

        def attention(S):
            nkb = 8 * S + 9
            h0 = nkb // 2
            tiles = list(range(nkb - 1, -1, -1))
            n = nkb
            seq = [(c, t) for c in range(8) for t in range(n)]
            G = len(seq)

            def half(kb):
                return "hi" if kb >= h0 else "lo"

            def issue_load(c, which):
                kv_load(S, c, which)

            def cs_of(t):
                rel = tiles[t] - (8 * S + 1)
                if rel < 0:
                    return 0, rel
                return 128 * (rel // 2), rel

            def S1(g):
                c, t = seq[g]
                kb = tiles[t]
                cs, rel = cs_of(t)
                for hh in range(2):
                    pr = slice(64 * hh, 64 * hh + 64)
                    mm(ps[:, hh, cs:512], kTs[pr, kb * 128:(kb + 1) * 128], qT[pr, c, cs:512], True, rel < 0,
                       reads=[("kTs", half(kb)), ("qT", c)], writes=[("ps", hh)])
                if rel >= 0:
                    for hh in range(2):
                        mm(ps[:, hh, cs:512], ident_sb[:, :], dmask3[:, rel, cs:512], False, True,
                           reads=["ident", "dmask"], writes=[("ps", hh)])

            def M(g):
                has_z = 0 <= g + 2 < G
                has_x = 0 <= g - 1 < G
                if not (has_z or has_x):
                    return
                k = g % 3
                rd, wr = [], []
                if has_z:
                    cz, _ = cs_of(seq[g + 2][1])
                    rd += [("ps", 0), ("ps", 1)]
                    wr.append(("EAe", k))
                if has_x:
                    cx, _ = cs_of(seq[g - 1][1])
                    rd += [("ps", 2), ("ps", 3)]
                    wr.append(("EAa", k))
                if has_z and has_x:
                    c0 = min(cz, cx)
                    o_, i_ = EA[k][:, :, c0:512], ps[:, 0:4, c0:512]
                elif has_z:
                    o_, i_ = EA[k][:, 0:2, cz:512], ps[:, 0:2, cz:512]
                else:
                    o_, i_ = EA[k][:, 2:4, cx:512], ps[:, 2:4, cx:512]
                sc.op("act", lambda e, o=o_, i=i_: e.activation(out=o, in_=i, func=AF.Exp), reads=rd, writes=wr)

            def S3(g):
                c, t = seq[g]
                cs, rel = cs_of(t)
                b = g % 2
                k = (g - 2) % 3
                sc.op("act", lambda e, o=sp2[b][:, :, cs:512], i=EA[k][:, 0:2, cs:512]:
                      e.activation(out=o, in_=i, func=AF.Ln, bias=one_ap, scale=1.0),
                      reads=[("EAe", k), "cst1"], writes=[("sp2", b)])

            def S4(g):
                c, t = seq[g]
                kb = tiles[t]
                cs, rel = cs_of(t)
                b = g % 2
                for hh in range(2):
                    pr = slice(64 * hh, 64 * hh + 64)
                    mm(ps[:, 2 + hh, cs:512], kTs[pr, kb * 128:(kb + 1) * 128], qT[pr, c, cs:512], True, False,
                       reads=[("kTs", half(kb)), ("qT", c)], writes=[("ps", 2 + hh)])
                for hh in range(2):
                    mm(ps[:, 2 + hh, cs:512], negT_sb[:, :], sp2[b][:, hh, cs:512], False, t == 0 and rel < 0,
                       reads=[("sp2", b), "negT"], writes=[("ps", 2 + hh)])
                if t > 0:
                    for hh in range(2):
                        mm(ps[:, 2 + hh, cs:512], negones_sb[:, :], Lsum[:, hh, cs:512], False, rel < 0,
                           reads=[("Lsum",), "negones"], writes=[("ps", 2 + hh)])
                if rel >= 0:
                    for hh in range(2):
                        mm(ps[:, 2 + hh, cs:512], ident_sb[:, :], dmask3[:, rel, cs:512], False, True,
                           reads=["ident", "dmask"], writes=[("ps", 2 + hh)])

            def UPD(g):
                c, t = seq[g]
                cs, rel = cs_of(t)
                b = g % 2
                if t == n - 1:
                    return
                if t == 0:
                    sc.op("pool", lambda e, o=Lsum[:, :, :]: e.memset(o, 0.0), writes=[("Lsum",)])
                sc.op("dve", lambda e, o=Lsum[:, :, cs:512], i=sp2[b][:, :, cs:512]:
                      e.tensor_tensor(out=o, in0=o, in1=i, op=ALU.add),
                      reads=[("sp2", b), ("Lsum",)], writes=[("Lsum",)])

            def S7(g):
                c, t = seq[g]
                kb = tiles[t]
                cs, rel = cs_of(t)
                b = g % 2
                Ob = 4 + 2 * (c % 2)
                if t == 0:
                    for hh in range(2):
                        mm(ps[:, Ob + hh, :], ztile[:, :], dmask3[:, 0, :], True, False,
                           reads=["ztile", "dmask"], writes=[("ps", Ob + hh)])
                ka = (g + 1) % 3
                for hh in range(2):
                    mm(ps[:, Ob + hh, cs:512], vs[:, kb, :], EA[ka][:, 2 + hh, cs:512], False, t == n - 1,
                       reads=[("vs", half(kb)), ("EAa", ka)], writes=[("ps", Ob + hh)])
                if kb == h0 and c + 1 < 8:
                    issue_load(c + 1, "hi")
                if t == n - 1:
                    if c + 1 < 8:
                        issue_load(c + 1, "lo")
                    for hh in range(2):
                        pr = slice(64 * hh, 64 * hh + 64)
                        sc.op("dve", lambda e, o=oT[pr, c, :], i=ps[pr, Ob + hh, :]: e.tensor_copy(out=o, in_=i),
                              reads=[("ps", Ob + hh)], writes=[("oT", c, hh)])

            S1(0)
            M(-2)
            if G > 1:
                S1(1)
            M(-1)
            if G > 2:
                S1(2)
            for g in range(G + 1):
                if g < G:
                    S3(g)
                M(g)
                if g + 3 < G:
                    S1(g + 3)
                if g < G:
                    S4(g)
                    UPD(g)
                if g >= 1:
                    S7(g - 1)

        def attn_layer(S, lb):
            l = 2 + lb
            sbase = 52 + 20 * lb
            T = 512
            nts = [(0, 512)]
            hok = lambda c, ni: [("ho", c)]
            barrier()
            kv_load(S, 0, "hi")
            kv_load(S, 0, "lo")
            cast_pre(hown, hok, hnB, "hn", T, nts)
            si = sbase
            for sj in range(2):
                slot = next_slab(si)
                si += 1
                sl = slab8[slot]
                for ocl in range(4):
                    oc = sj * 4 + ocl
                    b = acc()
                    for kc in range(8):
                        mm(ps[:, b, :], sl[:, kc, ocl * 128:(ocl + 1) * 128], hnB[:, kc, :], kc == 0, kc == 7,
                           reads=[("slab", slot), ("hn", kc, 0)], writes=[("ps", b)])
                    sc.op("dve", lambda e, o=qT[:, oc, :], i=ps[:, b, :]:
                          e.scalar_tensor_tensor(out=o, in0=i, scalar=0.125, in1=rstd[:, 0:512],
                                                 op0=ALU.mult, op1=ALU.mult),
                          reads=[("ps", b), ("rstd", 0)], writes=[("qT", oc)])
            attention(S)
            for sj in range(2):
                slot = next_slab(si)
                si += 1
                sl = slab8[slot]
                for ocl in range(4):
                    oc = sj * 4 + ocl
                    b = acc()
                    for kc in range(8):
                        mm(ps[:, b, :], sl[:, kc, ocl * 128:(ocl + 1) * 128], oT[:, kc, :], kc == 0, kc == 7,
                           reads=[("slab", slot), ("oT", kc, 0), ("oT", kc, 1)], writes=[("ps", b)])
                    sc.op("act", lambda e, o=mixB[:, oc, :], i=ps[:, b, :]:
                          e.activation(out=o, in_=i, func=AF.Copy),
                          reads=[("ps", b)], writes=[("mixB", oc, 0)])
                    post_stats_add(ps[:, b, :], [("ps", b)], oc, 0, 512)
            mk = lambda c, ni: [("mixB", c, 0)]
            post_stats_finish(nts)
            apply_post(l * 4 + 1, mixB, mk, hown, hok, T, nts)
            mlp(l, sbase + 4, hown, hok, mixB, "mixB", hnB, hidB, T, nts)

        def select_own():
            for c in range(8):
                hv = h[:, c, :].rearrange("p (i two q) -> p i two q", two=2, q=128)
                ev = hv[:, :, 0, :]
                od = hv[:, :, 1, :]
                dst = hown[:, c, :].rearrange("p (i q) -> p i q", q=128)
                j = st["tc"]
                st["tc"] ^= 1
                tmp = tmpc[j][:, 0:512].rearrange("p (i q) -> p i q", q=128)
                sc.op("dve", lambda e, o=tmp, i=ev: e.tensor_scalar(out=o, in0=i, scalar1=pid_sb[:, 0:1], scalar2=None,
                                                                op0=ALU.mult),
                      reads=[("h", c, 0), ("h", c, 1), "pid"], writes=[("tc", j)])
                sc.op("dve", lambda e, o=dst, i=od, t_=tmp:
                      e.scalar_tensor_tensor(out=o, in0=i, scalar=pid_sb[:, 1:2], in1=t_, op0=ALU.mult, op1=ALU.add),
                      reads=[("h", c, 0), ("h", c, 1), ("tc", j), "pid"], writes=[("ho", c)])

        sc.dma("sp", "x", h[:, :, 0:NMETA], metaT.rearrange("(c p) t -> p c t", p=128),
               writes=[("h", c, 0) for c in range(8)])
        mnts = [(0, NMETA)]
        conv_layer(0, NMETA, mnts)
        conv_layer(1, NMETA, mnts)
        kv_phase(NMETA, mnts, 112, 0, True)
        nts = [(0, 512), (512, 512)]
        for S in range(nsg):
            barrier()
            for ni in range(2):
                sc.dma("sp", "x", h[:, :, ni * 512:(ni + 1) * 512],
                       xT[:, S * 1024 + ni * 512:S * 1024 + (ni + 1) * 512].rearrange("(c p) t -> p c t", p=128),
                       writes=[("h", c, ni) for c in range(8)])
            conv_layer(0, 1024, nts)
            conv_layer(1, 1024, nts)
            kv_phase(1024, nts, 128 + S * 1024, 1 + 8 * S, False)
            barrier()
            select_own()
            barrier()
            attn_layer(S, 0)
            attn_layer(S, 1)
            sc.dma("pool", "out", outT[:, S * 512:(S + 1) * 512].rearrange("(c p) t -> p c t", p=128), hown[:, :, :],
                   reads=[("ho", c) for c in range(8)], writes=[("out", S)])
        barrier()
        assert st["slab_k"] == len(slab_seq), (st["slab_k"], len(slab_seq))

        with nc.Block() as block:
            @block.tensor
            def _(e):
                for f in sc.ops["pe"]:
                    f(e)

            @block.scalar
            def _(e):
                for f in sc.ops["act"]:
                    f(e)

            @block.vector
            def _(e):
                for f in sc.ops["dve"]:
                    f(e)

            @block.gpsimd
            def _(e):
                for f in sc.ops["pool"]:
                    f(e)

            @block.sync
            def _(e):
                for f in sc.ops["sp"]:
                    f(e)
    return nc


def _host_consts(p):
    bf = ml_dtypes.bfloat16
    j = np.arange(128)[:, None]
    s = np.arange(128)[None, :]
    negT = np.where(j >= s, -1.0, 0.0).astype(bf)
    negones = np.full((128, 128), -1.0, dtype=bf)
    ones = np.ones((128, 128), dtype=bf)
    dm = np.zeros((128, 8, 4, 128), dtype=np.float32)
    tri = (np.arange(128)[:, None] < np.arange(128)[None, :]).astype(np.float32)
    for rel in range(8):
        for i in range(4):
            qb = 2 * i + p
            if rel < qb:
                dm[:, rel, i, :] = 1.0
            elif rel == qb:
                dm[:, rel, i, :] = tri
    dmask = ((dm - 1.0) * 30000.0).reshape(128, 8 * 512).astype(bf)
    ident = np.eye(128, dtype=np.float32).astype(bf)
    pid = np.zeros((128, 2), dtype=np.float32)
    pid[:, 0] = 1.0 - p
    pid[:, 1] = float(p)
    return negT, negones, ones, dmask, pid, ident


def kernel(x, meta_tokens, norm_gains, conv_in_proj, conv_w, conv_out_proj, kv_norm,
           w_k, w_v, w_q, w_o, mlp_w1, mlp_w2):
    f32 = np.float32
    x = np.asarray(x, dtype=f32)
    B, SEQ, _ = x.shape
    nsg = SEQ // 1024
    assert B * 2 == 8 and SEQ == nsg * 1024
    nc = build(nsg)

    def pc(v):
        return np.asarray(v, dtype=f32).reshape(8, 128).T

    ng = np.asarray(norm_gains, dtype=f32)
    gl = [pc(ng[l, j]) for l in range(4) for j in range(4)] + [pc(np.asarray(kv_norm, dtype=f32))]
    gains = np.ascontiguousarray(np.concatenate(gl, axis=1))
    cw = np.asarray(conv_w, dtype=f32)
    convw = np.ascontiguousarray(np.concatenate([pc(cw[l, t]) for l in range(2) for t in range(3)], axis=1))
    metaT = np.ascontiguousarray(np.asarray(meta_tokens, dtype=f32).T)
    shared = {
        "metaT": metaT, "gains": gains, "convw": convw,
        "conv_in_proj": np.ascontiguousarray(conv_in_proj, dtype=f32),
        "conv_out_proj": np.ascontiguousarray(conv_out_proj, dtype=f32),
        "w_k": np.ascontiguousarray(w_k, dtype=f32), "w_v": np.ascontiguousarray(w_v, dtype=f32),
        "w_q": np.ascontiguousarray(w_q, dtype=f32), "w_o": np.ascontiguousarray(w_o, dtype=f32),
        "mlp_w1": np.ascontiguousarray(mlp_w1, dtype=f32), "mlp_w2": np.ascontiguousarray(mlp_w2, dtype=f32),
    }
    consts = [_host_consts(p) for p in range(2)]
    xTs = [np.ascontiguousarray(x[b].T) for b in range(B)]
    in_maps = []
    for core in range(8):
        b, p = core // 2, core % 2
        negT, negones, ones, dmask, pid, ident = consts[p]
        m = dict(shared)
        m.update({"xT": xTs[b], "pid": pid, "dmask": dmask, "negT": negT, "negones": negones, "ones": ones,
                  "ident": ident})
        in_maps.append(m)
    res = run_bass_kernel_spmd(nc, in_maps, core_ids=list(range(8)))
    out = np.empty((B, SEQ, D), dtype=f32)
    for core in range(8):
        b, p = core // 2, core % 2
        oT = np.asarray(res.results[core]["outT"], dtype=f32)
        o = oT.T.reshape(nsg, 4, 128, D)
        ov = out[b].reshape(nsg, 4, 2, 128, D)
        ov[:, :, p] = o
    return out
```

```python
import numpy as np
import ml_dtypes
from contextlib import ExitStack
import concourse.bass as bass
import concourse.mybir as mybir
from concourse.bass_utils import run_bass_kernel_spmd

F32 = mybir.dt.float32
BF16 = mybir.dt.bfloat16
AF = mybir.ActivationFunctionType
ALU = mybir.AluOpType

D = 1024
NMETA = 16
NSLOT = 4
ENGS = ["pe", "act", "dve", "pool", "sp"]


class _Sem:
    def __init__(self, h, i):
        self.h = h
        self.i = i


class Sched:
    def __init__(self, new_sem):
        self.new_sem = new_sem
        self.ops = {e: [] for e in ENGS}
        self.cnt = {e: 0 for e in ENGS}
        self.prog = {e: new_sem("prog_" + e) for e in ENGS}
        self.waited = {e: {} for e in ENGS}
        self.lastw = {}
        self.readers = {}
        self.pools = {}
        self.n_wait = 0

    def add_pool(self, name, n):
        self.pools[name] = {"sems": [[self.new_sem("%s%d" % (name, i)), 0] for i in range(n)], "next": 0}

    def _need(self, E, tok):
        sem, val, eng = tok
        if eng == E and E == "pe":
            return
        w = self.waited[E]
        if w.get(sem.i, 0) >= val:
            return
        w[sem.i] = val
        h = sem.h
        self.n_wait += 1
        self.ops[E].append(lambda e, h=h, val=val: e.wait_ge(h, val))

    def _deps(self, E, reads, writes):
        for k in reads:
            t = self.lastw.get(k)
            if t is not None:
                self._need(E, t)
        for k in writes:
            t = self.lastw.get(k)
            if t is not None:
                self._need(E, t)
            r = self.readers.get(k)
            if r:
                for t in r.values():
                    self._need(E, t)

    def _commit(self, tok, reads, writes):
        for k in writes:
            self.lastw[k] = tok
            self.readers[k] = {}
        for k in reads:
            d = self.readers.setdefault(k, {})
            old = d.get(tok[0].i)
            if old is None or old[1] < tok[1]:
                d[tok[0].i] = tok

    def op(self, E, fn, reads=(), writes=()):
        self._deps(E, reads, writes)
        self.cnt[E] += 1
        sem = self.prog[E]
        h = sem.h
        self.ops[E].append(lambda e, fn=fn, h=h: fn(e).then_inc(h, 1))
        tok = (sem, self.cnt[E], E)
        self._commit(tok, reads, writes)
        return tok

    def dma(self, Q, pool, out_ap, in_ap, reads=(), writes=()):
        self._deps(Q, reads, writes)
        p = self.pools[pool]
        ent = p["sems"][p["next"] % len(p["sems"])]
        p["next"] += 1
        sem = ent[0]
        if ent[1] > 0:
            self._need(Q, (sem, 16 * ent[1], None))
        ent[1] += 1
        h = sem.h
        self.ops[Q].append(lambda e, o=out_ap, i=in_ap, h=h: e.dma_start(out=o, in_=i).then_inc(h, 16))
        tok = (sem, 16 * ent[1], None)
        self._commit(tok, reads, writes)
        return tok

    def barrier(self, dummy):
        M = "pool"
        for E in ENGS:
            if E != M and self.cnt[E] > 0:
                self._need(M, (self.prog[E], self.cnt[E], E))
        for p in self.pools.values():
            for ent in p["sems"]:
                if ent[1] > 0:
                    self._need(M, (ent[0], 16 * ent[1], None))
        prev = self.lastw.get("_bar")
        if prev is not None:
            self._need(M, prev)
        tok = self.op(M, lambda e: e.memset(dummy, 0.0))
        for E in ENGS:
            if E != M:
                self._need(E, tok)
        self.lastw.clear()
        self.readers.clear()
        self.lastw["_bar"] = tok


def build(nsg):
    nc = bass.Bass("TRN2", target_bir_lowering=False)
    TSEQ = nsg * 1024
    NKB = 8 * nsg + 1
    NK = NKB * 128

    def din(name, shape, dt=F32):
        return nc.dram_tensor(name, list(shape), dt, kind="ExternalInput").ap()

    xT = din("xT", [D, TSEQ])
    metaT = din("metaT", [D, NMETA])
    gains_d = din("gains", [128, 17 * 8])
    convw_d = din("convw", [128, 48])
    pid_d = din("pid", [128, 2])
    dmask_d = din("dmask", [128, 8 * 512], BF16)
    negT_d = din("negT", [128, 128], BF16)
    negones_d = din("negones", [128, 128], BF16)
    ones_d = din("ones", [128, 128], BF16)
    ident_d = din("ident", [128, 128], BF16)
    cin = din("conv_in_proj", [2, D, 3 * D])
    cout = din("conv_out_proj", [2, D, D])
    wk = din("w_k", [D, D])
    wv = din("w_v", [D, D])
    wq = din("w_q", [2, D, D])
    wo = din("w_o", [2, D, D])
    w1 = din("mlp_w1", [4, D, 4 * D])
    w2 = din("mlp_w2", [4, 4 * D, D])
    outT = nc.dram_tensor("outT", [D, nsg * 512], F32, kind="ExternalOutput").ap()

    wsl = nc.dram_tensor("wsl", [92, 128, 4096], BF16).ap()
    kTd = nc.dram_tensor("kTd", [8, 128, NK], BF16).ap()
    vd = nc.dram_tensor("vd", [8, 128, NKB, 128], BF16).ap()

    def k1024(W, j):
        return W.rearrange("(kc p) n -> p kc n", p=128)[:, :, j * 512:(j + 1) * 512]

    def w2slab(W, hh, j):
        return W.rearrange("(hh kc p) n -> hh p kc n", hh=2, p=128)[hh][:, :, j * 256:(j + 1) * 256]

    slab_src = []

    def add_mlp(l):
        for hh in range(2):
            for j in range(4):
                slab_src.append((k1024(w1[l], hh * 4 + j), 8))
            for j in range(4):
                slab_src.append((w2slab(w2[l], hh, j), 16))

    INPERM = [0, 1, 2, 4, 3, 5]
    for l in range(2):
        for j in range(6):
            slab_src.append((k1024(cin[l], INPERM[j]), 8))
        for j in range(2):
            slab_src.append((k1024(cout[l], j), 8))
        add_mlp(l)
    for j in range(2):
        slab_src.append((k1024(wk, j), 8))
    for j in range(2):
        slab_src.append((k1024(wv, j), 8))
    for lb in range(2):
        for j in range(2):
            slab_src.append((k1024(wq[lb], j), 8))
        for j in range(2):
            slab_src.append((k1024(wo[lb], j), 8))
        add_mlp(2 + lb)
    assert len(slab_src) == 92

    slab_seq = list(range(52))
    for S in range(nsg):
        slab_seq += list(range(92))

    with ExitStack() as es:
        def sb(name, shape, dt):
            return es.enter_context(nc.sbuf_tensor(name, list(shape), dt))

        sem_count = [0]

        def new_sem(name):
            h = es.enter_context(nc.semaphore(name))
            sem_count[0] += 1
            return _Sem(h, sem_count[0])

        slabs = [sb("slab%d" % i, [128, 4096], BF16) for i in range(NSLOT)]
        R1 = sb("R1", [128, 8192], F32)
        R2 = sb("R2", [128, 8192], F32)
        hn_t = sb("hn", [128, 8192], BF16)
        sqt = [sb("sq%d" % i, [128, 1024], BF16) for i in range(2)]
        lnv = sb("lnv", [128, 1024], F32)
        rstd = sb("rstd", [128, 1024], F32)
        tmpc = [sb("tmpc%d" % i, [128, 1024], F32) for i in range(2)]
        dmask_sb = sb("dmask_sb", [128, 4096], BF16)
        gains_sb = sb("gains_sb", [128, 17 * 8], F32)
        convw_sb = sb("convw_sb", [128, 48], F32)
        pid_sb = sb("pid_sb", [128, 2], F32)
        negT_sb = sb("negT_sb", [128, 128], BF16)
        negones_sb = sb("negones_sb", [128, 128], BF16)
        ones_sb = sb("ones_sb", [128, 128], BF16)
        ident_sb = sb("ident_sb", [128, 128], BF16)
        ztile = sb("ztile", [128, 128], BF16)
        cst = sb("cst", [128, 4], F32)
        ust = sb("ust", [128, 32], F32)
        dummy = sb("dummyt", [128, 4], F32)
        arena = sb("arena", [128, 16384], F32)
        ps = es.enter_context(nc.psum_tensor("ps", [128, 8, 512], F32))

        sc = Sched(new_sem)
        for i in range(NSLOT):
            sc.add_pool("slab%d" % i, 1)
        sc.add_pool("x", 2)
        sc.add_pool("const", 4)
        sc.add_pool("pl", 3)
        sc.add_pool("pst", 2)
        sc.add_pool("kvst", 8)
        sc.add_pool("kvld", 4)
        sc.add_pool("out", 2)

        def carve(off, n, dt):
            if dt == F32:
                return arena[:, off // 4: off // 4 + n]
            return arena[:, off // 4: off // 4 + n // 2].bitcast(BF16)

        h = R1[:, :].rearrange("p (c t) -> p c t", c=8)
        mix = R2[:, :].rearrange("p (c t) -> p c t", c=8)
        hn = hn_t[:, :].rearrange("p (c t) -> p c t", c=8)
        hown = R2[:, 0:4096].rearrange("p (c t) -> p c t", c=8)
        mixB = R2[:, 4096:8192].rearrange("p (c t) -> p c t", c=8)
        hnB = hn_t[:, 0:4096].rearrange("p (c t) -> p c t", c=8)
        qT = R1[:, 0:2048].bitcast(BF16).rearrange("p (c t) -> p c t", c=8)
        oT = R1[:, 2048:4096].bitcast(BF16).rearrange("p (c t) -> p c t", c=8)
        slab8 = [s[:, :].rearrange("p (k o) -> p k o", k=8) for s in slabs]
        slab16 = [s[:, :].rearrange("p (k o) -> p k o", k=16) for s in slabs]
        dmask3 = dmask_sb[:, :].rearrange("p (r t) -> p r t", r=8)
        ust3l = [ust[:, l * 16:(l + 1) * 16].rearrange("p (c t) -> p c t", c=8) for l in range(2)]
        eps_ap = cst[:, 0:1]
        one_ap = cst[:, 1:2]

        bgate = carve(0, 8192, BF16).rearrange("p (c t) -> p c t", c=8)
        cy = carve(16384, 8192, BF16).rearrange("p (c t) -> p c t", c=8)
        ub = [carve(32768 + i * 4112, 1028, F32) for i in range(2)]
        ctmp = [carve(41216 + i * 4096, 1024, F32) for i in range(2)]
        hid = carve(0, 16384, BF16).rearrange("p (c t) -> p c t", c=16)
        hidB = carve(0, 8192, BF16).rearrange("p (c t) -> p c t", c=16)
        rtmp = [carve(32768 + i * 2048, 512, F32) for i in range(2)]
        ktmp = [carve(i * 1024, 512, BF16) for i in range(2)]
        vtmp = [carve(2048 + i * 1024, 512, BF16) for i in range(2)]
        kTs = carve(0, 8320, BF16)
        vs = carve(16640, 8320, BF16).rearrange("p (k f) -> p k f", f=128)
        e2 = [carve(33280 + i * 4096, 1024, F32).rearrange("p (a t) -> p a t", a=2) for i in range(2)]
        sp2 = [carve(41472 + i * 2048, 1024, BF16).rearrange("p (a t) -> p a t", a=2) for i in range(2)]
        A2 = [carve(45568 + i * 2048, 1024, BF16).rearrange("p (a t) -> p a t", a=2) for i in range(2)]
        Lsum = carve(49664, 1024, BF16).rearrange("p (a t) -> p a t", a=2)
        stg = [carve(i * 16384, 4096, F32) for i in range(3)]
        cb = [carve(49152 + i * 8192, 4096, BF16) for i in range(2)]

        st = {"acc": 0, "alt": 0, "sq": 0, "slab_k": 0, "slab_loaded": 0, "rt": 0, "kt": 0, "vt": 0, "tc": 0}

        def acc():
            st["acc"] = (st["acc"] + 1) % 4
            return st["acc"]

        def alt():
            st["alt"] ^= 1
            return "dve" if st["alt"] else "pool"

        def barrier():
            sc.barrier(dummy[:, 0:1])

        for dst, src, key in [(gains_sb, gains_d, "gains"), (convw_sb, convw_d, "convw"), (pid_sb, pid_d, "pid"),
                              (dmask_sb, dmask_d, "dmask"), (negT_sb, negT_d, "negT"),
                              (negones_sb, negones_d, "negones"), (ones_sb, ones_d, "ones"),
                              (ident_sb, ident_d, "ident")]:
            sc.dma("sp", "const", dst[:, :], src, writes=[key])
        sc.op("pool", lambda e: e.memset(ztile[:, :], 0.0), writes=["ztile"])
        sc.op("pool", lambda e: e.memset(cst[:, 0:1], 1e-6), writes=["cst0"])
        sc.op("pool", lambda e: e.memset(cst[:, 1:2], 1.0), writes=["cst1"])
        sc.op("pool", lambda e: e.memset(ust[:, :], 0.0), writes=[("ust", l_, f) for l_ in range(2) for f in range(8)])
        kT_keys = [[] for _ in range(8)]
        v_keys = [[] for _ in range(8)]
        for c in range(8):
            sc.dma("pool", "kvst", kTd[c, :, 0:112], ztile[:, 0:112], reads=["ztile"], writes=[("kTd", c, "z")])
            kT_keys[c].append(("kTd", c, "z"))
            sc.dma("pool", "kvst", vd[c, 0:112, 0, :], ztile[0:112, :], reads=["ztile"], writes=[("vd", c, "z")])
            v_keys[c].append(("vd", c, "z"))

        fold = {}
        for l_ in range(4):
            base_ = (24 * l_ + 8) if l_ < 2 else (52 + 20 * (l_ - 2) + 4)
            for hh_ in range(2):
                for j_ in range(4):
                    fold[base_ + hh_ * 8 + j_] = l_ * 4 + 2
        for lb_ in range(2):
            for j_ in range(2):
                fold[52 + 20 * lb_ + j_] = (2 + lb_) * 4 + 0
        for s, (src, kc) in enumerate(slab_src):
            i3 = s % 3
            i2 = s % 2
            dst = stg[i3].rearrange("p (k o) -> p k o", k=kc)
            sc.dma("sp" if s % 2 == 0 else "act", "pl", dst, src, writes=[("stg", i3)])
            eng = "dve" if s % 2 == 0 else "pool"
            if s in fold:
                gi = fold[s]
                for kq in range(8):
                    sc.op("dve", lambda e, o=cb[i2][:, kq * 512:(kq + 1) * 512], i=stg[i3][:, kq * 512:(kq + 1) * 512],
                          g=gains_sb[:, gi * 8 + kq:gi * 8 + kq + 1]:
                          e.tensor_scalar(out=o, in0=i, scalar1=g, scalar2=None, op0=ALU.mult),
                          reads=[("stg", i3), "gains"], writes=[("cb", i2)])
            else:
                sc.op(eng, lambda e, o=cb[i2], i=stg[i3]: e.tensor_copy(out=o, in_=i),
                      reads=[("stg", i3)], writes=[("cb", i2)])
            sc.dma("pool", "pst", wsl[s], cb[i2], reads=[("cb", i2)], writes=[("wsl", s)])
        barrier()

        def issue_slab_loads(upto):
            while st["slab_loaded"] <= upto and st["slab_loaded"] < len(slab_seq):
                k = st["slab_loaded"]
                slot = k % NSLOT
                sc.dma("sp", "slab%d" % slot, slabs[slot][:, :], wsl[slab_seq[k]], writes=[("slab", slot)])
                st["slab_loaded"] += 1

        def next_slab(expect, hold=0):
            k = st["slab_k"]
            assert slab_seq[k] == expect, (k, slab_seq[k], expect)
            issue_slab_loads(k + NSLOT - 1 - hold)
            st["slab_k"] += 1
            return k % NSLOT

        def mm(out, lhsT, rhs, start, stop, reads, writes):
            sc.op("pe", lambda e: e.matmul(out, lhsT=lhsT, rhs=rhs, start=start, stop=stop),
                  reads=reads, writes=writes)

        def norm_stats(src, skeys, T, nts):
            for ni, (n0, N) in enumerate(nts):
                for c in range(8):
                    j = st["sq"]
                    st["sq"] ^= 1
                    sc.op("act", lambda e, o=sqt[j][:, :N], i=src[:, c, n0:n0 + N]:
                          e.activation(out=o, in_=i, func=AF.Square),
                          reads=skeys(c, ni), writes=[("sq", j)])
                    mm(ps[:, 4 + ni, :N], ones_sb[:, :], sqt[j][:, :N], c == 0, c == 7,
                       reads=[("sq", j), "ones"], writes=[("ps", 4 + ni)])
                sc.op("act", lambda e, o=lnv[:, n0:n0 + N], i=ps[:, 4 + ni, :N]:
                      e.activation(out=o, in_=i, func=AF.Ln, bias=eps_ap, scale=1.0 / D),
                      reads=[("ps", 4 + ni), "cst0"], writes=[("lnv", ni)])
                sc.op("act", lambda e, o=rstd[:, n0:n0 + N], i=lnv[:, n0:n0 + N]:
                      e.activation(out=o, in_=i, func=AF.Exp, scale=-0.5),
                      reads=[("lnv", ni)], writes=[("rstd", ni)])

        def apply_pre(gi, src, skeys, dst, dkey, T, nts):
            for ni, (n0, N) in enumerate(nts):
                for c in range(8):
                    g = gains_sb[:, gi * 8 + c: gi * 8 + c + 1]
                    sc.op("dve", lambda e, o=dst[:, c, n0:n0 + N], i=src[:, c, n0:n0 + N], g=g,
                          r=rstd[:, n0:n0 + N]:
                          e.scalar_tensor_tensor(out=o, in0=i, scalar=g, in1=r, op0=ALU.mult, op1=ALU.mult),
                          reads=skeys(c, ni) + [("rstd", ni), "gains"], writes=[(dkey, c, ni)])

        def apply_post(gi, src, skeys, res, rkeys, T, nts):
            for ni, (n0, N) in enumerate(nts):
                for c in range(8):
                    g = gains_sb[:, gi * 8 + c: gi * 8 + c + 1]
                    j = st["tc"]
                    st["tc"] ^= 1
                    sc.op("dve", lambda e, o=tmpc[j][:, :N], i=src[:, c, n0:n0 + N], g=g, r=rstd[:, n0:n0 + N]:
                          e.scalar_tensor_tensor(out=o, in0=i, scalar=g, in1=r, op0=ALU.mult, op1=ALU.mult),
                          reads=skeys(c, ni) + [("rstd", ni), "gains"], writes=[("tc", j)])
                    sc.op("pool" if c % 2 == 0 else "dve", lambda e, o=res[:, c, n0:n0 + N], i=tmpc[j][:, :N]:
                          e.tensor_tensor(out=o, in0=o, in1=i, op=ALU.add),
                          reads=[("tc", j)] + rkeys(c, ni), writes=rkeys(c, ni))

        def cast_pre(src, skeys, dst, dkey, T, nts):
            for ni, (n0, N) in enumerate(nts):
                for c in range(8):
                    sc.op("act", lambda e, o=dst[:, c, n0:n0 + N], i=src[:, c, n0:n0 + N]:
                          e.activation(out=o, in_=i, func=AF.Copy),
                          reads=skeys(c, ni), writes=[(dkey, c, ni)])
            norm_stats(src, skeys, T, nts)

        def order(first, nts):
            if first and len(nts) > 1:
                return [(ocl, ni) for ni in range(len(nts)) for ocl in range(4)]
            return [(ocl, ni) for ocl in range(4) for ni in range(len(nts))]

        def slab_plan(nslabs, nts):
            plan = []
            j0 = 0
            if len(nts) > 1 and nslabs >= 2:
                for ni in range(len(nts)):
                    for jj in range(2):
                        for ocl in range(4):
                            plan.append((jj, ocl, ni))
                j0 = 2
            for jj in range(j0, nslabs):
                for ocl in range(4):
                    for ni in range(len(nts)):
                        plan.append((jj, ocl, ni))
            return plan

        pend = []

        def _emit_stat(item):
            j, oc, ni, N = item
            mm(ps[:, 4 + ni, :N], ones_sb[:, :], sqt[j][:, :N], oc == 0, oc == 7,
               reads=[("sq", j), "ones"], writes=[("ps", 4 + ni)])

        def post_stats_add(src_ap, src_keys, oc, ni, N):
            j = st["sq"]
            st["sq"] ^= 1
            sc.op("act", lambda e, o=sqt[j][:, :N], i=src_ap: e.activation(out=o, in_=i, func=AF.Square),
                  reads=src_keys, writes=[("sq", j)])
            pend.append((j, oc, ni, N))
            while len(pend) > 1:
                _emit_stat(pend.pop(0))

        def post_stats_finish(nts):
            while pend:
                _emit_stat(pend.pop(0))
            for ni, (n0, N) in enumerate(nts):
                sc.op("act", lambda e, o=lnv[:, n0:n0 + N], i=ps[:, 4 + ni, :N]:
                      e.activation(out=o, in_=i, func=AF.Ln, bias=eps_ap, scale=1.0 / D),
                      reads=[("ps", 4 + ni), "cst0"], writes=[("lnv", ni)])
                sc.op("act", lambda e, o=rstd[:, n0:n0 + N], i=lnv[:, n0:n0 + N]:
                      e.activation(out=o, in_=i, func=AF.Exp, scale=-0.5),
                      reads=[("lnv", ni)], writes=[("rstd", ni)])

        def mlp(l, sbase, res, rkeys, mx, mkey, hnv, hidv, T, nts):
            barrier()
            cast_pre(res, rkeys, hnv, "hn", T, nts)
            for ni, (n0, N) in enumerate(nts):
                sc.op("dve", lambda e, o=lnv[:, n0:n0 + N], i=rstd[:, n0:n0 + N]:
                      e.tensor_tensor(out=o, in0=i, in1=i, op=ALU.mult),
                      reads=[("rstd", ni), ("lnv", ni)], writes=[("lnv", ni)])
            si = sbase
            for hh in range(2):
                slots = {}
                for jj, ocl, ni in slab_plan(4, nts) if hh == 0 else [(jj, ocl, ni) for jj in range(4)
                                                                      for ocl in range(4) for ni in range(len(nts))]:
                    if jj not in slots:
                        for k_ in range(len(slots), jj + 1):
                            slots[k_] = next_slab(si, hold=1 if (k_ == 1 and len(nts) > 1) else 0)
                            si += 1
                    slot = slots[jj]
                    sl = slab8[slot]
                    n0, N = nts[ni]
                    hc = jj * 4 + ocl
                    b = acc()
                    for kc in range(8):
                        mm(ps[:, b, :N], sl[:, kc, ocl * 128:(ocl + 1) * 128], hnv[:, kc, n0:n0 + N],
                           kc == 0, kc == 7, reads=[("slab", slot), ("hn", kc, ni)], writes=[("ps", b)])
                    ri = st["rt"]
                    st["rt"] ^= 1
                    sc.op("act", lambda e, o=rtmp[ri][:, :N], i=ps[:, b, :N]:
                          e.activation(out=o, in_=i, func=AF.Relu),
                          reads=[("ps", b)], writes=[("rt", ri)])
                    sc.op("act", lambda e, o=rtmp[ri][:, :N]: e.activation(out=o, in_=o, func=AF.Square),
                          reads=[("rt", ri)], writes=[("rt", ri)])
                    sc.op(alt(), lambda e, o=hidv[:, hc, n0:n0 + N], i=rtmp[ri][:, :N], r2=lnv[:, n0:n0 + N]:
                          e.tensor_tensor(out=o, in0=i, in1=r2, op=ALU.mult),
                          reads=[("rt", ri), ("lnv", ni)], writes=[("hid", hc, ni)])
                for j in range(4):
                    slot = next_slab(si)
                    si += 1
                    sl = slab16[slot]
                    for ocl in range(2):
                        oc = j * 2 + ocl
                        for ni, (n0, N) in enumerate(nts):
                            b = acc()
                            for kc in range(16):
                                mm(ps[:, b, :N], sl[:, kc, ocl * 128:(ocl + 1) * 128], hidv[:, kc, n0:n0 + N],
                                   kc == 0, kc == 15, reads=[("slab", slot), ("hid", kc, ni)], writes=[("ps", b)])
                            if hh == 0:
                                sc.op("act", lambda e, o=mx[:, oc, n0:n0 + N], i=ps[:, b, :N]:
                                      e.activation(out=o, in_=i, func=AF.Copy),
                                      reads=[("ps", b)], writes=[(mkey, oc, ni)])
                            else:
                                sc.op("dve", lambda e, o=mx[:, oc, n0:n0 + N], i=ps[:, b, :N]:
                                      e.tensor_tensor(out=o, in0=i, in1=o, op=ALU.add),
                                      reads=[("ps", b), (mkey, oc, ni)], writes=[(mkey, oc, ni)])
                                post_stats_add(mx[:, oc, n0:n0 + N], [(mkey, oc, ni)], oc, ni, N)
            mk = lambda c, ni: [(mkey, c, ni)]
            post_stats_finish(nts)
            apply_post(l * 4 + 3, mx, mk, res, rkeys, T, nts)

        def conv_layer(l, T, nts):
            nn = len(nts)
            sbase = 24 * l
            hk = lambda c, ni: [("h", c, ni)]
            barrier()
            norm_stats(h, hk, T, nts)
            apply_pre(l * 4 + 0, h, hk, hn, "hn", T, nts)
            si = sbase
            slots = {}
            done = {}
            for sj, ocl, ni in slab_plan(6, nts):
                if True:
                    if sj not in slots:
                        for k_ in range(len(slots), sj + 1):
                            slots[k_] = next_slab(si, hold=1 if (k_ == 1 and len(nts) > 1) else 0)
                            si += 1
                    slot = slots[sj]
                    sl = slab8[slot]
                    n0, N = nts[ni]
                    oc = INPERM[sj] * 4 + ocl
                    f = oc % 8
                    j = f % 2
                    b = acc()
                    for kc in range(8):
                        mm(ps[:, b, :N], sl[:, kc, ocl * 128:(ocl + 1) * 128], hn[:, kc, n0:n0 + N],
                           kc == 0, kc == 7, reads=[("slab", slot), ("hn", kc, ni)], writes=[("ps", b)])
                    if oc < 8:
                        sc.op("act", lambda e, o=bgate[:, f, n0:n0 + N], i=ps[:, b, :N]:
                              e.activation(out=o, in_=i, func=AF.Copy),
                              reads=[("ps", b)], writes=[("bg", f, ni)])
                    elif oc < 16:
                        sc.op("act", lambda e, o=cy[:, f, n0:n0 + N], i=ps[:, b, :N]:
                              e.activation(out=o, in_=i, func=AF.Copy),
                              reads=[("ps", b)], writes=[("cy", f, ni)])
                    else:
                        sc.op("dve", lambda e, o=ub[j][:, 2 + n0:2 + n0 + N], i=ps[:, b, :N],
                              c_=cy[:, f, n0:n0 + N]: e.tensor_tensor(out=o, in0=i, in1=c_, op=ALU.mult),
                              reads=[("ps", b), ("cy", f, ni)], writes=[("ub", j, ni)])
                    done[oc] = done.get(oc, 0) + 1
                    if oc >= 16 and done[oc] == nn:
                        ubk = [("ub", j, n_) for n_ in range(nn)]
                        sc.op("pool", lambda e, o=ub[j][:, 0:2], i=ust3l[l][:, f, :]: e.tensor_copy(out=o, in_=i),
                              reads=[("ust", l, f)], writes=[("ubs", j)])
                        cwi = lambda tap: convw_sb[:, (l * 3 + tap) * 8 + f:(l * 3 + tap) * 8 + f + 1]
                        sc.op("act", lambda e, o=ctmp[j][:, :T], i=ub[j][:, 0:T], s_=cwi(0):
                              e.activation(out=o, in_=i, func=AF.Copy, scale=s_),
                              reads=ubk + [("ubs", j), "convw"], writes=[("ct", j)])
                        for tap in (1, 2):
                            sc.op("dve", lambda e, o=ctmp[j][:, :T], i=ub[j][:, tap:tap + T], s_=cwi(tap):
                                  e.scalar_tensor_tensor(out=o, in0=i, scalar=s_, in1=o, op0=ALU.mult, op1=ALU.add),
                                  reads=ubk + [("ubs", j), ("ct", j), "convw"], writes=[("ct", j)])
                        sc.op("pool" if f % 2 == 0 else "dve",
                              lambda e, o=cy[:, f, :T], i=ctmp[j][:, :T], g=bgate[:, f, :T]:
                              e.tensor_tensor(out=o, in0=i, in1=g, op=ALU.mult),
                              reads=[("ct", j)] + [("bg", f, n_) for n_ in range(nn)],
                              writes=[("cy", f, n_) for n_ in range(nn)])
                        sc.op("pool", lambda e, o=ust3l[l][:, f, :], i=ub[j][:, T:T + 2]: e.tensor_copy(out=o, in_=i),
                              reads=ubk + [("ubs", j)], writes=[("ust", l, f)])
            for sj in range(2):
                slot = next_slab(si)
                si += 1
                sl = slab8[slot]
                for ocl in range(4):
                    oc = sj * 4 + ocl
                    for ni, (n0, N) in enumerate(nts):
                        b = acc()
                        for kc in range(8):
                            mm(ps[:, b, :N], sl[:, kc, ocl * 128:(ocl + 1) * 128], cy[:, kc, n0:n0 + N],
                               kc == 0, kc == 7, reads=[("slab", slot), ("cy", kc, ni)], writes=[("ps", b)])
                        sc.op("act", lambda e, o=mix[:, oc, n0:n0 + N], i=ps[:, b, :N]:
                              e.activation(out=o, in_=i, func=AF.Copy),
                              reads=[("ps", b)], writes=[("mix", oc, ni)])
                        post_stats_add(ps[:, b, :N], [("ps", b)], oc, ni, N)
            mk = lambda c, ni: [("mix", c, ni)]
            post_stats_finish(nts)
            apply_post(l * 4 + 1, mix, mk, h, hk, T, nts)
            mlp(l, sbase + 8, h, hk, mix, "mix", hn, hid, T, nts)

        def kv_phase(T, nts, kcol0, kb0, meta):
            hk = lambda c, ni: [("h", c, ni)]
            barrier()
            norm_stats(h, hk, T, nts)
            apply_pre(16, h, hk, hn, "hn", T, nts)
            si = 48
            for sj in range(2):
                slot = next_slab(si)
                si += 1
                sl = slab8[slot]
                for ocl, ni in order(sj == 0, nts):
                    n0, N = nts[ni]
                    oc = sj * 4 + ocl
                    b = acc()
                    for kc in range(8):
                        mm(ps[:, b, :N], sl[:, kc, ocl * 128:(ocl + 1) * 128], hn[:, kc, n0:n0 + N],
                           kc == 0, kc == 7, reads=[("slab", slot), ("hn", kc, ni)], writes=[("ps", b)])
                    ki = st["kt"]
                    st["kt"] ^= 1
                    sc.op("act" if ni == 0 else "dve",
                          (lambda e, o=ktmp[ki][:, :N], i=ps[:, b, :N]: e.activation(out=o, in_=i, func=AF.Copy))
                          if ni == 0 else
                          (lambda e, o=ktmp[ki][:, :N], i=ps[:, b, :N]: e.tensor_copy(out=o, in_=i)),
                          reads=[("ps", b)], writes=[("kt", ki)])
                    key = ("kTd", oc, kcol0 + n0)
                    sc.dma("pool", "kvst", kTd[oc, :, kcol0 + n0:kcol0 + n0 + N], ktmp[ki][:, :N],
                           reads=[("kt", ki)], writes=[key])
                    kT_keys[oc].append(key)
            for sj in range(2):
                slot = next_slab(si)
                si += 1
                sl = slab8[slot]
                nblk = 1 if meta else T // 128
                for jb in range(nblk):
                    M = NMETA if meta else 128
                    t0 = jb * 128
                    ni = t0 // 512
                    p0 = 112 if meta else 0
                    kb = kb0 + jb
                    b = acc()
                    for kc in range(8):
                        mm(ps[:M, b, :], hn[:, kc, t0:t0 + M], sl[:, kc, :], kc == 0, kc == 7,
                           reads=[("slab", slot), ("hn", kc, ni)], writes=[("ps", b)])
                    vi = st["vt"]
                    st["vt"] ^= 1
                    sc.op("act" if jb % 2 == 0 else "dve",
                          (lambda e, o=vtmp[vi][:M, :], i=ps[:M, b, :]: e.activation(out=o, in_=i, func=AF.Copy))
                          if jb % 2 == 0 else
                          (lambda e, o=vtmp[vi][:M, :], i=ps[:M, b, :]: e.tensor_copy(out=o, in_=i)),
                          reads=[("ps", b)], writes=[("vt", vi)])
                    dst = vd[sj * 4:(sj + 1) * 4, p0:p0 + M, kb, :].rearrange("c p f -> p c f")
                    srcv = vtmp[vi][:M, :].rearrange("p (c f) -> p c f", c=4)
                    keys = [("vd", sj * 4 + cc, kb) for cc in range(4)]
                    sc.dma("pool", "kvst", dst, srcv, reads=[("vt", vi)], writes=keys)
                    for cc in range(4):
                        v_keys[sj * 4 + cc].append(keys[cc])

        def kv_load(S, c, which):
            nkb = 8 * S + 9
            h0 = nkb // 2
            a_, bnd = (h0, nkb) if which == "hi" else (0, h0)
            sc.dma("pool", "kvld", kTs[:, a_ * 128:bnd * 128], kTd[c, :, a_ * 128:bnd * 128],
                   reads=list(kT_keys[c]), writes=[("kTs", which)])
            sc.dma("pool", "kvld", vs[:, a_:bnd, :], vd[c, :, a_:bnd, :],
                   reads=list(v_keys[c]), writes=[("vs", which)])

        def attention(S):
            nkb = 8 * S + 9
            h0 = nkb // 2
            tiles = list(range(nkb - 1, -1, -1))
            n = nkb
            seq = [(c, t) for c in range(8) for t in range(n)]
            G = len(seq)

            def half(kb):
                return "hi" if kb >= h0 else "lo"

            def issue_load(c, which):
                kv_load(S, c, which)

            def cs_of(t):
                rel = tiles[t] - (8 * S + 1)
                if rel < 0:
                    return 0, rel
                return 128 * (rel // 2), rel

            def S1(g):
                c, t = seq[g]
                kb = tiles[t]
                cs, rel = cs_of(t)
                for hh in range(2):
                    pr = slice(64 * hh, 64 * hh + 64)
                    mm(ps[:, hh, cs:512], kTs[pr, kb * 128:(kb + 1) * 128], qT[pr, c, cs:512], True, rel < 0,
                       reads=[("kTs", half(kb)), ("qT", c)], writes=[("ps", hh)])
                if rel >= 0:
                    for hh in range(2):
                        mm(ps[:, hh, cs:512], ident_sb[:, :], dmask3[:, rel, cs:512], False, True,
                           reads=["ident", "dmask"], writes=[("ps", hh)])

            def S2(g):
                c, t = seq[g]
                cs, rel = cs_of(t)
                b = g % 2
                sc.op("act", lambda e, o=e2[b][:, :, cs:512], i=ps[:, 0:2, cs:512]:
                      e.activation(out=o, in_=i, func=AF.Exp),
                      reads=[("ps", 0), ("ps", 1)], writes=[("e2", b)])

            def S3(g):
                c, t = seq[g]
                cs, rel = cs_of(t)
                b = g % 2
                sc.op("act", lambda e, o=sp2[b][:, :, cs:512], i=e2[b][:, :, cs:512]:
                      e.activation(out=o, in_=i, func=AF.Ln, bias=one_ap, scale=1.0),
                      reads=[("e2", b), "cst1"], writes=[("sp2", b)])

            def S4(g):
                c, t = seq[g]
                kb = tiles[t]
                cs, rel = cs_of(t)
                b = g % 2
                for hh in range(2):
                    pr = slice(64 * hh, 64 * hh + 64)
                    mm(ps[:, 2 + hh, cs:512], kTs[pr, kb * 128:(kb + 1) * 128], qT[pr, c, cs:512], True, False,
                       reads=[("kTs", half(kb)), ("qT", c)], writes=[("ps", 2 + hh)])
                for hh in range(2):
                    mm(ps[:, 2 + hh, cs:512], negT_sb[:, :], sp2[b][:, hh, cs:512], False, t == 0 and rel < 0,
                       reads=[("sp2", b), "negT"], writes=[("ps", 2 + hh)])
                if t > 0:
                    for hh in range(2):
                        mm(ps[:, 2 + hh, cs:512], negones_sb[:, :], Lsum[:, hh, cs:512], False, rel < 0,
                           reads=[("Lsum",), "negones"], writes=[("ps", 2 + hh)])
                if rel >= 0:
                    for hh in range(2):
                        mm(ps[:, 2 + hh, cs:512], ident_sb[:, :], dmask3[:, rel, cs:512], False, True,
                           reads=["ident", "dmask"], writes=[("ps", 2 + hh)])

            def UPD(g):
                c, t = seq[g]
                cs, rel = cs_of(t)
                b = g % 2
                if t == n - 1:
                    return
                if t == 0:
                    sc.op("pool", lambda e, o=Lsum[:, :, :]: e.memset(o, 0.0), writes=[("Lsum",)])
                sc.op("dve", lambda e, o=Lsum[:, :, cs:512], i=sp2[b][:, :, cs:512]:
                      e.tensor_tensor(out=o, in0=o, in1=i, op=ALU.add),
                      reads=[("sp2", b), ("Lsum",)], writes=[("Lsum",)])

            def S6(g):
                c, t = seq[g]
                cs, rel = cs_of(t)
                b = g % 2
                sc.op("act", lambda e, o=A2[b][:, :, cs:512], i=ps[:, 2:4, cs:512]:
                      e.activation(out=o, in_=i, func=AF.Exp),
                      reads=[("ps", 2), ("ps", 3)], writes=[("A2", b)])

            def S7(g):
                c, t = seq[g]
                kb = tiles[t]
                cs, rel = cs_of(t)
                b = g % 2
                Ob = 4 + 2 * (c % 2)
                if t == 0:
                    for hh in range(2):
                        mm(ps[:, Ob + hh, :], ztile[:, :], dmask3[:, 0, :], True, False,
                           reads=["ztile", "dmask"], writes=[("ps", Ob + hh)])
                for hh in range(2):
                    mm(ps[:, Ob + hh, cs:512], vs[:, kb, :], A2[b][:, hh, cs:512], False, t == n - 1,
                       reads=[("vs", half(kb)), ("A2", b)], writes=[("ps", Ob + hh)])
                if kb == h0 and c + 1 < 8:
                    issue_load(c + 1, "hi")
                if t == n - 1:
                    if c + 1 < 8:
                        issue_load(c + 1, "lo")
                    for hh in range(2):
                        pr = slice(64 * hh, 64 * hh + 64)
                        sc.op("dve", lambda e, o=oT[pr, c, :], i=ps[pr, Ob + hh, :]: e.tensor_copy(out=o, in_=i),
                              reads=[("ps", Ob + hh)], writes=[("oT", c, hh)])

            S1(0)
            S2(0)
            if G > 1:
                S1(1)
            for g in range(G + 1):
                if g + 1 < G:
                    S2(g + 1)
                if g < G:
                    S3(g)
                if g >= 1:
                    S6(g - 1)
                if g + 2 < G:
                    S1(g + 2)
                if g < G:
                    S4(g)
                    UPD(g)
                if g >= 1:
                    S7(g - 1)

        def attn_layer(S, lb):
            l = 2 + lb
            sbase = 52 + 20 * lb
            T = 512
            nts = [(0, 512)]
            hok = lambda c, ni: [("ho", c)]
            barrier()
            kv_load(S, 0, "hi")
            kv_load(S, 0, "lo")
            cast_pre(hown, hok, hnB, "hn", T, nts)
            si = sbase
            for sj in range(2):
                slot = next_slab(si)
                si += 1
                sl = slab8[slot]
                for ocl in range(4):
                    oc = sj * 4 + ocl
                    b = acc()
                    for kc in range(8):
                        mm(ps[:, b, :], sl[:, kc, ocl * 128:(ocl + 1) * 128], hnB[:, kc, :], kc == 0, kc == 7,
                           reads=[("slab", slot), ("hn", kc, 0)], writes=[("ps", b)])
                    sc.op("dve", lambda e, o=qT[:, oc, :], i=ps[:, b, :]:
                          e.scalar_tensor_tensor(out=o, in0=i, scalar=0.125, in1=rstd[:, 0:512],
                                                 op0=ALU.mult, op1=ALU.mult),
                          reads=[("ps", b), ("rstd", 0)], writes=[("qT", oc)])
            attention(S)
            for sj in range(2):
                slot = next_slab(si)
                si += 1
                sl = slab8[slot]
                for ocl in range(4):
                    oc = sj * 4 + ocl
                    b = acc()
                    for kc in range(8):
                        mm(ps[:, b, :], sl[:, kc, ocl * 128:(ocl + 1) * 128], oT[:, kc, :], kc == 0, kc == 7,
                           reads=[("slab", slot), ("oT", kc, 0), ("oT", kc, 1)], writes=[("ps", b)])
                    sc.op("act", lambda e, o=mixB[:, oc, :], i=ps[:, b, :]:
                          e.activation(out=o, in_=i, func=AF.Copy),
                          reads=[("ps", b)], writes=[("mixB", oc, 0)])
                    post_stats_add(ps[:, b, :], [("ps", b)], oc, 0, 512)
            mk = lambda c, ni: [("mixB", c, 0)]
            post_stats_finish(nts)
            apply_post(l * 4 + 1, mixB, mk, hown, hok, T, nts)
            mlp(l, sbase + 4, hown, hok, mixB, "mixB", hnB, hidB, T, nts)

        def select_own():
            for c in range(8):
                hv = h[:, c, :].rearrange("p (i two q) -> p i two q", two=2, q=128)
                ev = hv[:, :, 0, :]
                od = hv[:, :, 1, :]
                dst = hown[:, c, :].rearrange("p (i q) -> p i q", q=128)
                j = st["tc"]
                st["tc"] ^= 1
                tmp = tmpc[j][:, 0:512].rearrange("p (i q) -> p i q", q=128)
                sc.op("dve", lambda e, o=tmp, i=ev: e.tensor_scalar(out=o, in0=i, scalar1=pid_sb[:, 0:1], scalar2=None,
                                                                op0=ALU.mult),
                      reads=[("h", c, 0), ("h", c, 1), "pid"], writes=[("tc", j)])
                sc.op("dve", lambda e, o=dst, i=od, t_=tmp:
                      e.scalar_tensor_tensor(out=o, in0=i, scalar=pid_sb[:, 1:2], in1=t_, op0=ALU.mult, op1=ALU.add),
                      reads=[("h", c, 0), ("h", c, 1), ("tc", j), "pid"], writes=[("ho", c)])

        sc.dma("sp", "x", h[:, :, 0:NMETA], metaT.rearrange("(c p) t -> p c t", p=128),
               writes=[("h", c, 0) for c in range(8)])
        mnts = [(0, NMETA)]
        conv_layer(0, NMETA, mnts)
        conv_layer(1, NMETA, mnts)
        kv_phase(NMETA, mnts, 112, 0, True)
        nts = [(0, 512), (512, 512)]
        for S in range(nsg):
            barrier()
            for ni in range(2):
                sc.dma("sp", "x", h[:, :, ni * 512:(ni + 1) * 512],
                       xT[:, S * 1024 + ni * 512:S * 1024 + (ni + 1) * 512].rearrange("(c p) t -> p c t", p=128),
                       writes=[("h", c, ni) for c in range(8)])
            conv_layer(0, 1024, nts)
            conv_layer(1, 1024, nts)
            kv_phase(1024, nts, 128 + S * 1024, 1 + 8 * S, False)
            barrier()
            select_own()
            barrier()
            attn_layer(S, 0)
            attn_layer(S, 1)
            sc.dma("pool", "out", outT[:, S * 512:(S + 1) * 512].rearrange("(c p) t -> p c t", p=128), hown[:, :, :],
                   reads=[("ho", c) for c in range(8)], writes=[("out", S)])
        barrier()
        assert st["slab_k"] == len(slab_seq), (st["slab_k"], len(slab_seq))

        with nc.Block() as block:
            @block.tensor
            def _(e):
                for f in sc.ops["pe"]:
                    f(e)

            @block.scalar
            def _(e):
                for f in sc.ops["act"]:
                    f(e)

            @block.vector
            def _(e):
                for f in sc.ops["dve"]:
                    f(e)

            @block.gpsimd
            def _(e):
                for f in sc.ops["pool"]:
                    f(e)

            @block.sync
            def _(e):
                for f in sc.ops["sp"]:
                    f(e)
    return nc


def _host_consts(p):
    bf = ml_dtypes.bfloat16
    j = np.arange(128)[:, None]
    s = np.arange(128)[None, :]
    negT = np.where(j >= s, -1.0, 0.0).astype(bf)
    negones = np.full((128, 128), -1.0, dtype=bf)
    ones = np.ones((128, 128), dtype=bf)
    dm = np.zeros((128, 8, 4, 128), dtype=np.float32)
    tri = (np.arange(128)[:, None] < np.arange(128)[None, :]).astype(np.float32)
    for rel in range(8):
        for i in range(4):
            qb = 2 * i + p
            if rel < qb:
                dm[:, rel, i, :] = 1.0
            elif rel == qb:
                dm[:, rel, i, :] = tri
    dmask = ((dm - 1.0) * 30000.0).reshape(128, 8 * 512).astype(bf)
    ident = np.eye(128, dtype=np.float32).astype(bf)
    pid = np.zeros((128, 2), dtype=np.float32)
    pid[:, 0] = 1.0 - p
    pid[:, 1] = float(p)
    return negT, negones, ones, dmask, pid, ident


def kernel(x, meta_tokens, norm_gains, conv_in_proj, conv_w, conv_out_proj, kv_norm,
           w_k, w_v, w_q, w_o, mlp_w1, mlp_w2):
    f32 = np.float32
    x = np.asarray(x, dtype=f32)
    B, SEQ, _ = x.shape
    nsg = SEQ // 1024
    assert B * 2 == 8 and SEQ == nsg * 1024
    nc = build(nsg)

    def pc(v):
        return np.asarray(v, dtype=f32).reshape(8, 128).T

    ng = np.asarray(norm_gains, dtype=f32)
    gl = [pc(ng[l, j]) for l in range(4) for j in range(4)] + [pc(np.asarray(kv_norm, dtype=f32))]
    gains = np.ascontiguousarray(np.concatenate(gl, axis=1))
    cw = np.asarray(conv_w, dtype=f32)
    convw = np.ascontiguousarray(np.concatenate([pc(cw[l, t]) for l in range(2) for t in range(3)], axis=1))
    metaT = np.ascontiguousarray(np.asarray(meta_tokens, dtype=f32).T)
    shared = {
        "metaT": metaT, "gains": gains, "convw": convw,
        "conv_in_proj": np.ascontiguousarray(conv_in_proj, dtype=f32),
        "conv_out_proj": np.ascontiguousarray(conv_out_proj, dtype=f32),
        "w_k": np.ascontiguousarray(w_k, dtype=f32), "w_v": np.ascontiguousarray(w_v, dtype=f32),
        "w_q": np.ascontiguousarray(w_q, dtype=f32), "w_o": np.ascontiguousarray(w_o, dtype=f32),
        "mlp_w1": np.ascontiguousarray(mlp_w1, dtype=f32), "mlp_w2": np.ascontiguousarray(mlp_w2, dtype=f32),
    }
    consts = [_host_consts(p) for p in range(2)]
    xTs = [np.ascontiguousarray(x[b].T) for b in range(B)]
    in_maps = []
    for core in range(8):
        b, p = core // 2, core % 2
        negT, negones, ones, dmask, pid, ident = consts[p]
        m = dict(shared)
        m.update({"xT": xTs[b], "pid": pid, "dmask": dmask, "negT": negT, "negones": negones, "ones": ones,
                  "ident": ident})
        in_maps.append(m)
    res = run_bass_kernel_spmd(nc, in_maps, core_ids=list(range(8)))
    out = np.empty((B, SEQ, D), dtype=f32)
    for core in range(8):
        b, p = core // 2, core % 2
        oT = np.asarray(res.results[core]["outT"], dtype=f32)
        o = oT.T.reshape(nsg, 4, 128, D)
        ov = out[b].reshape(nsg, 4, 2, 128, D)
        ov[:, :, p] = o
    return out
```

```python
import numpy as np
import ml_dtypes
from contextlib import ExitStack
import concourse.bass as bass
import concourse.mybir as mybir
from concourse.bass_utils import run_bass_kernel_spmd

F32 = mybir.dt.float32
BF16 = mybir.dt.bfloat16
AF = mybir.ActivationFunctionType
ALU = mybir.AluOpType

D = 1024
NMETA = 16
NSLOT = 4
ENGS = ["pe", "act", "dve", "pool", "sp"]


class _Sem:
    def __init__(self, h, i):
        self.h = h
        self.i = i


class Sched:
    def __init__(self, new_sem):
        self.new_sem = new_sem
        self.ops = {e: [] for e in ENGS}
        self.cnt = {e: 0 for e in ENGS}
        self.prog = {e: new_sem("prog_" + e) for e in ENGS}
        self.waited = {e: {} for e in ENGS}
        self.lastw = {}
        self.readers = {}
        self.pools = {}
        self.n_wait = 0

    def add_pool(self, name, n):
        self.pools[name] = {"sems": [[self.new_sem("%s%d" % (name, i)), 0] for i in range(n)], "next": 0}

    def _need(self, E, tok):
        sem, val, eng = tok
        if eng == E and E == "pe":
            return
        w = self.waited[E]
        if w.get(sem.i, 0) >= val:
            return
        w[sem.i] = val
        h = sem.h
        self.n_wait += 1
        self.ops[E].append(lambda e, h=h, val=val: e.wait_ge(h, val))

    def _deps(self, E, reads, writes):
        for k in reads:
            t = self.lastw.get(k)
            if t is not None:
                self._need(E, t)
        for k in writes:
            t = self.lastw.get(k)
            if t is not None:
                self._need(E, t)
            r = self.readers.get(k)
            if r:
                for t in r.values():
                    self._need(E, t)

    def _commit(self, tok, reads, writes):
        for k in writes:
            self.lastw[k] = tok
            self.readers[k] = {}
        for k in reads:
            d = self.readers.setdefault(k, {})
            old = d.get(tok[0].i)
            if old is None or old[1] < tok[1]:
                d[tok[0].i] = tok

    def op(self, E, fn, reads=(), writes=()):
        self._deps(E, reads, writes)
        self.cnt[E] += 1
        sem = self.prog[E]
        h = sem.h
        self.ops[E].append(lambda e, fn=fn, h=h: fn(e).then_inc(h, 1))
        tok = (sem, self.cnt[E], E)
        self._commit(tok, reads, writes)
        return tok

    def dma(self, Q, pool, out_ap, in_ap, reads=(), writes=()):
        self._deps(Q, reads, writes)
        p = self.pools[pool]
        ent = p["sems"][p["next"] % len(p["sems"])]
        p["next"] += 1
        sem = ent[0]
        if ent[1] > 0:
            self._need(Q, (sem, 16 * ent[1], None))
        ent[1] += 1
        h = sem.h
        self.ops[Q].append(lambda e, o=out_ap, i=in_ap, h=h: e.dma_start(out=o, in_=i).then_inc(h, 16))
        tok = (sem, 16 * ent[1], None)
        self._commit(tok, reads, writes)
        return tok

    def barrier(self, dummy):
        M = "pool"
        for E in ENGS:
            if E != M and self.cnt[E] > 0:
                self._need(M, (self.prog[E], self.cnt[E], E))
        for p in self.pools.values():
            for ent in p["sems"]:
                if ent[1] > 0:
                    self._need(M, (ent[0], 16 * ent[1], None))
        prev = self.lastw.get("_bar")
        if prev is not None:
            self._need(M, prev)
        tok = self.op(M, lambda e: e.memset(dummy, 0.0))
        for E in ENGS:
            if E != M:
                self._need(E, tok)
        self.lastw.clear()
        self.readers.clear()
        self.lastw["_bar"] = tok


def build(nsg):
    nc = bass.Bass("TRN2", target_bir_lowering=False)
    TSEQ = nsg * 1024
    NKB = 8 * nsg + 1
    NK = NKB * 128

    def din(name, shape, dt=F32):
        return nc.dram_tensor(name, list(shape), dt, kind="ExternalInput").ap()

    xT = din("xT", [D, TSEQ])
    metaT = din("metaT", [D, NMETA])
    gains_d = din("gains", [128, 17 * 8])
    convw_d = din("convw", [128, 48])
    pid_d = din("pid", [128, 2])
    dmask_d = din("dmask", [128, 8 * 512], BF16)
    negT_d = din("negT", [128, 128], BF16)
    negones_d = din("negones", [128, 128], BF16)
    ones_d = din("ones", [128, 128], BF16)
    ident_d = din("ident", [128, 128], BF16)
    cin = din("conv_in_proj", [2, D, 3 * D])
    cout = din("conv_out_proj", [2, D, D])
    wk = din("w_k", [D, D])
    wv = din("w_v", [D, D])
    wq = din("w_q", [2, D, D])
    wo = din("w_o", [2, D, D])
    w1 = din("mlp_w1", [4, D, 4 * D])
    w2 = din("mlp_w2", [4, 4 * D, D])
    outT = nc.dram_tensor("outT", [D, nsg * 512], F32, kind="ExternalOutput").ap()

    wsl = nc.dram_tensor("wsl", [92, 128, 4096], BF16).ap()
    kTd = nc.dram_tensor("kTd", [8, 128, NK], BF16).ap()
    vd = nc.dram_tensor("vd", [8, 128, NKB, 128], BF16).ap()

    def k1024(W, j):
        return W.rearrange("(kc p) n -> p kc n", p=128)[:, :, j * 512:(j + 1) * 512]

    def w2slab(W, hh, j):
        return W.rearrange("(hh kc p) n -> hh p kc n", hh=2, p=128)[hh][:, :, j * 256:(j + 1) * 256]

    slab_src = []

    def add_mlp(l):
        for hh in range(2):
            for j in range(4):
                slab_src.append((k1024(w1[l], hh * 4 + j), 8))
            for j in range(4):
                slab_src.append((w2slab(w2[l], hh, j), 16))

    INPERM = [0, 1, 2, 4, 3, 5]
    for l in range(2):
        for j in range(6):
            slab_src.append((k1024(cin[l], INPERM[j]), 8))
        for j in range(2):
            slab_src.append((k1024(cout[l], j), 8))
        add_mlp(l)
    for j in range(2):
        slab_src.append((k1024(wk, j), 8))
    for j in range(2):
        slab_src.append((k1024(wv, j), 8))
    for lb in range(2):
        for j in range(2):
            slab_src.append((k1024(wq[lb], j), 8))
        for j in range(2):
            slab_src.append((k1024(wo[lb], j), 8))
        add_mlp(2 + lb)
    assert len(slab_src) == 92

    slab_seq = list(range(52))
    for S in range(nsg):
        slab_seq += list(range(92))

    with ExitStack() as es:
        def sb(name, shape, dt):
            return es.enter_context(nc.sbuf_tensor(name, list(shape), dt))

        sem_count = [0]

        def new_sem(name):
            h = es.enter_context(nc.semaphore(name))
            sem_count[0] += 1
            return _Sem(h, sem_count[0])

        slabs = [sb("slab%d" % i, [128, 4096], BF16) for i in range(NSLOT)]
        R1 = sb("R1", [128, 8192], F32)
        R2 = sb("R2", [128, 8192], F32)
        hn_t = sb("hn", [128, 8192], BF16)
        sqt = [sb("sq%d" % i, [128, 1024], BF16) for i in range(2)]
        lnv = sb("lnv", [128, 1024], F32)
        rstd = sb("rstd", [128, 1024], F32)
        tmpc = [sb("tmpc%d" % i, [128, 1024], F32) for i in range(2)]
        dmask_sb = sb("dmask_sb", [128, 4096], BF16)
        gains_sb = sb("gains_sb", [128, 17 * 8], F32)
        convw_sb = sb("convw_sb", [128, 48], F32)
        pid_sb = sb("pid_sb", [128, 2], F32)
        negT_sb = sb("negT_sb", [128, 128], BF16)
        negones_sb = sb("negones_sb", [128, 128], BF16)
        ones_sb = sb("ones_sb", [128, 128], BF16)
        ident_sb = sb("ident_sb", [128, 128], BF16)
        ztile = sb("ztile", [128, 128], BF16)
        cst = sb("cst", [128, 4], F32)
        ust = sb("ust", [128, 32], F32)
        dummy = sb("dummyt", [128, 4], F32)
        arena = sb("arena", [128, 16384], F32)
        ps = es.enter_context(nc.psum_tensor("ps", [128, 8, 512], F32))

        sc = Sched(new_sem)
        for i in range(NSLOT):
            sc.add_pool("slab%d" % i, 1)
        sc.add_pool("x", 2)
        sc.add_pool("const", 4)
        sc.add_pool("pl", 3)
        sc.add_pool("pst", 2)
        sc.add_pool("kvst", 8)
        sc.add_pool("kvld", 4)
        sc.add_pool("out", 2)

        def carve(off, n, dt):
            if dt == F32:
                return arena[:, off // 4: off // 4 + n]
            return arena[:, off // 4: off // 4 + n // 2].bitcast(BF16)

        h = R1[:, :].rearrange("p (c t) -> p c t", c=8)
        mix = R2[:, :].rearrange("p (c t) -> p c t", c=8)
        hn = hn_t[:, :].rearrange("p (c t) -> p c t", c=8)
        hown = R2[:, 0:4096].rearrange("p (c t) -> p c t", c=8)
        mixB = R2[:, 4096:8192].rearrange("p (c t) -> p c t", c=8)
        hnB = hn_t[:, 0:4096].rearrange("p (c t) -> p c t", c=8)
        qT = R1[:, 0:2048].bitcast(BF16).rearrange("p (c t) -> p c t", c=8)
        oT = R1[:, 2048:4096].bitcast(BF16).rearrange("p (c t) -> p c t", c=8)
        slab8 = [s[:, :].rearrange("p (k o) -> p k o", k=8) for s in slabs]
        slab16 = [s[:, :].rearrange("p (k o) -> p k o", k=16) for s in slabs]
        dmask3 = dmask_sb[:, :].rearrange("p (r t) -> p r t", r=8)
        ust3l = [ust[:, l * 16:(l + 1) * 16].rearrange("p (c t) -> p c t", c=8) for l in range(2)]
        eps_ap = cst[:, 0:1]
        one_ap = cst[:, 1:2]

        bgate = carve(0, 8192, BF16).rearrange("p (c t) -> p c t", c=8)
        cy = carve(16384, 8192, BF16).rearrange("p (c t) -> p c t", c=8)
        ub = [carve(32768 + i * 4112, 1028, F32) for i in range(2)]
        ctmp = [carve(41216 + i * 4096, 1024, F32) for i in range(2)]
        hid = carve(0, 16384, BF16).rearrange("p (c t) -> p c t", c=16)
        hidB = carve(0, 8192, BF16).rearrange("p (c t) -> p c t", c=16)
        rtmp = [carve(32768 + i * 2048, 512, F32) for i in range(2)]
        ktmp = [carve(i * 1024, 512, BF16) for i in range(2)]
        vtmp = [carve(2048 + i * 1024, 512, BF16) for i in range(2)]
        kTs = carve(0, 8320, BF16)
        vs = carve(16640, 8320, BF16).rearrange("p (k f) -> p k f", f=128)
        e2 = [carve(33280 + i * 4096, 1024, F32).rearrange("p (a t) -> p a t", a=2) for i in range(2)]
        sp2 = [carve(41472 + i * 2048, 1024, BF16).rearrange("p (a t) -> p a t", a=2) for i in range(2)]
        A2 = [carve(45568 + i * 2048, 1024, BF16).rearrange("p (a t) -> p a t", a=2) for i in range(2)]
        Lsum = carve(49664, 1024, BF16).rearrange("p (a t) -> p a t", a=2)
        stg = [carve(i * 16384, 4096, F32) for i in range(3)]
        cb = [carve(49152 + i * 8192, 4096, BF16) for i in range(2)]

        st = {"acc": 0, "alt": 0, "sq": 0, "slab_k": 0, "slab_loaded": 0, "rt": 0, "kt": 0, "vt": 0, "tc": 0}

        def acc():
            st["acc"] = (st["acc"] + 1) % 4
            return st["acc"]

        def alt():
            st["alt"] ^= 1
            return "dve" if st["alt"] else "pool"

        def barrier():
            sc.barrier(dummy[:, 0:1])

        for dst, src, key in [(gains_sb, gains_d, "gains"), (convw_sb, convw_d, "convw"), (pid_sb, pid_d, "pid"),
                              (dmask_sb, dmask_d, "dmask"), (negT_sb, negT_d, "negT"),
                              (negones_sb, negones_d, "negones"), (ones_sb, ones_d, "ones"),
                              (ident_sb, ident_d, "ident")]:
            sc.dma("sp", "const", dst[:, :], src, writes=[key])
        sc.op("pool", lambda e: e.memset(ztile[:, :], 0.0), writes=["ztile"])
        sc.op("pool", lambda e: e.memset(cst[:, 0:1], 1e-6), writes=["cst0"])
        sc.op("pool", lambda e: e.memset(cst[:, 1:2], 1.0), writes=["cst1"])
        sc.op("pool", lambda e: e.memset(ust[:, :], 0.0), writes=[("ust", l_, f) for l_ in range(2) for f in range(8)])
        kT_keys = [[] for _ in range(8)]
        v_keys = [[] for _ in range(8)]
        for c in range(8):
            sc.dma("pool", "kvst", kTd[c, :, 0:112], ztile[:, 0:112], reads=["ztile"], writes=[("kTd", c, "z")])
            kT_keys[c].append(("kTd", c, "z"))
            sc.dma("pool", "kvst", vd[c, 0:112, 0, :], ztile[0:112, :], reads=["ztile"], writes=[("vd", c, "z")])
            v_keys[c].append(("vd", c, "z"))

        fold = {}
        for l_ in range(4):
            base_ = (24 * l_ + 8) if l_ < 2 else (52 + 20 * (l_ - 2) + 4)
            for hh_ in range(2):
                for j_ in range(4):
                    fold[base_ + hh_ * 8 + j_] = l_ * 4 + 2
        for lb_ in range(2):
            for j_ in range(2):
                fold[52 + 20 * lb_ + j_] = (2 + lb_) * 4 + 0
        for l_ in range(2):
            for j_ in range(6):
                fold[24 * l_ + j_] = l_ * 4 + 0
        for s, (src, kc) in enumerate(slab_src):
            i3 = s % 3
            i2 = s % 2
            dst = stg[i3].rearrange("p (k o) -> p k o", k=kc)
            sc.dma("sp" if s % 2 == 0 else "act", "pl", dst, src, writes=[("stg", i3)])
            eng = "dve" if s % 2 == 0 else "pool"
            if s in fold:
                gi = fold[s]
                for kq in range(8):
                    sc.op("dve", lambda e, o=cb[i2][:, kq * 512:(kq + 1) * 512], i=stg[i3][:, kq * 512:(kq + 1) * 512],
                          g=gains_sb[:, gi * 8 + kq:gi * 8 + kq + 1]:
                          e.tensor_scalar(out=o, in0=i, scalar1=g, scalar2=None, op0=ALU.mult),
                          reads=[("stg", i3), "gains"], writes=[("cb", i2)])
            else:
                sc.op(eng, lambda e, o=cb[i2], i=stg[i3]: e.tensor_copy(out=o, in_=i),
                      reads=[("stg", i3)], writes=[("cb", i2)])
            sc.dma("pool", "pst", wsl[s], cb[i2], reads=[("cb", i2)], writes=[("wsl", s)])
        barrier()

        def issue_slab_loads(upto):
            while st["slab_loaded"] <= upto and st["slab_loaded"] < len(slab_seq):
                k = st["slab_loaded"]
                slot = k % NSLOT
                sc.dma("sp", "slab%d" % slot, slabs[slot][:, :], wsl[slab_seq[k]], writes=[("slab", slot)])
                st["slab_loaded"] += 1

        def next_slab(expect, hold=0):
            k = st["slab_k"]
            assert slab_seq[k] == expect, (k, slab_seq[k], expect)
            issue_slab_loads(k + NSLOT - 1 - hold)
            st["slab_k"] += 1
            return k % NSLOT

        def mm(out, lhsT, rhs, start, stop, reads, writes):
            sc.op("pe", lambda e: e.matmul(out, lhsT=lhsT, rhs=rhs, start=start, stop=stop),
                  reads=reads, writes=writes)

        def norm_stats(src, skeys, T, nts):
            for ni, (n0, N) in enumerate(nts):
                for c in range(8):
                    j = st["sq"]
                    st["sq"] ^= 1
                    sc.op("act", lambda e, o=sqt[j][:, :N], i=src[:, c, n0:n0 + N]:
                          e.activation(out=o, in_=i, func=AF.Square),
                          reads=skeys(c, ni), writes=[("sq", j)])
                    mm(ps[:, 4 + ni, :N], ones_sb[:, :], sqt[j][:, :N], c == 0, c == 7,
                       reads=[("sq", j), "ones"], writes=[("ps", 4 + ni)])
                sc.op("act", lambda e, o=lnv[:, n0:n0 + N], i=ps[:, 4 + ni, :N]:
                      e.activation(out=o, in_=i, func=AF.Ln, bias=eps_ap, scale=1.0 / D),
                      reads=[("ps", 4 + ni), "cst0"], writes=[("lnv", ni)])
                sc.op("act", lambda e, o=rstd[:, n0:n0 + N], i=lnv[:, n0:n0 + N]:
                      e.activation(out=o, in_=i, func=AF.Exp, scale=-0.5),
                      reads=[("lnv", ni)], writes=[("rstd", ni)])

        def apply_pre(gi, src, skeys, dst, dkey, T, nts):
            for ni, (n0, N) in enumerate(nts):
                for c in range(8):
                    g = gains_sb[:, gi * 8 + c: gi * 8 + c + 1]
                    sc.op("dve", lambda e, o=dst[:, c, n0:n0 + N], i=src[:, c, n0:n0 + N], g=g,
                          r=rstd[:, n0:n0 + N]:
                          e.scalar_tensor_tensor(out=o, in0=i, scalar=g, in1=r, op0=ALU.mult, op1=ALU.mult),
                          reads=skeys(c, ni) + [("rstd", ni), "gains"], writes=[(dkey, c, ni)])

        def apply_post(gi, src, skeys, res, rkeys, T, nts):
            for ni, (n0, N) in enumerate(nts):
                for c in range(8):
                    g = gains_sb[:, gi * 8 + c: gi * 8 + c + 1]
                    j = st["tc"]
                    st["tc"] ^= 1
                    sc.op("dve", lambda e, o=tmpc[j][:, :N], i=src[:, c, n0:n0 + N], g=g, r=rstd[:, n0:n0 + N]:
                          e.scalar_tensor_tensor(out=o, in0=i, scalar=g, in1=r, op0=ALU.mult, op1=ALU.mult),
                          reads=skeys(c, ni) + [("rstd", ni), "gains"], writes=[("tc", j)])
                    sc.op("pool" if c % 2 == 0 else "dve", lambda e, o=res[:, c, n0:n0 + N], i=tmpc[j][:, :N]:
                          e.tensor_tensor(out=o, in0=o, in1=i, op=ALU.add),
                          reads=[("tc", j)] + rkeys(c, ni), writes=rkeys(c, ni))

        def cast_pre(src, skeys, dst, dkey, T, nts):
            for ni, (n0, N) in enumerate(nts):
                for c in range(8):
                    sc.op("act", lambda e, o=dst[:, c, n0:n0 + N], i=src[:, c, n0:n0 + N]:
                          e.activation(out=o, in_=i, func=AF.Copy),
                          reads=skeys(c, ni), writes=[(dkey, c, ni)])
            norm_stats(src, skeys, T, nts)

        def order(first, nts):
            if first and len(nts) > 1:
                return [(ocl, ni) for ni in range(len(nts)) for ocl in range(4)]
            return [(ocl, ni) for ocl in range(4) for ni in range(len(nts))]

        def slab_plan(nslabs, nts):
            plan = []
            j0 = 0
            if len(nts) > 1 and nslabs >= 2:
                for ni in range(len(nts)):
                    for jj in range(2):
                        for ocl in range(4):
                            plan.append((jj, ocl, ni))
                j0 = 2
            for jj in range(j0, nslabs):
                for ocl in range(4):
                    for ni in range(len(nts)):
                        plan.append((jj, ocl, ni))
            return plan

        pend = []

        def _emit_stat(item):
            j, oc, ni, N = item
            mm(ps[:, 4 + ni, :N], ones_sb[:, :], sqt[j][:, :N], oc == 0, oc == 7,
               reads=[("sq", j), "ones"], writes=[("ps", 4 + ni)])

        def post_stats_add(src_ap, src_keys, oc, ni, N):
            j = st["sq"]
            st["sq"] ^= 1
            sc.op("act", lambda e, o=sqt[j][:, :N], i=src_ap: e.activation(out=o, in_=i, func=AF.Square),
                  reads=src_keys, writes=[("sq", j)])
            pend.append((j, oc, ni, N))
            while len(pend) > 1:
                _emit_stat(pend.pop(0))

        def post_stats_finish(nts):
            while pend:
                _emit_stat(pend.pop(0))
            for ni, (n0, N) in enumerate(nts):
                sc.op("act", lambda e, o=lnv[:, n0:n0 + N], i=ps[:, 4 + ni, :N]:
                      e.activation(out=o, in_=i, func=AF.Ln, bias=eps_ap, scale=1.0 / D),
                      reads=[("ps", 4 + ni), "cst0"], writes=[("lnv", ni)])
                sc.op("act", lambda e, o=rstd[:, n0:n0 + N], i=lnv[:, n0:n0 + N]:
                      e.activation(out=o, in_=i, func=AF.Exp, scale=-0.5),
                      reads=[("lnv", ni)], writes=[("rstd", ni)])

        def mlp(l, sbase, res, rkeys, mx, mkey, hnv, hidv, T, nts):
            barrier()
            cast_pre(res, rkeys, hnv, "hn", T, nts)
            for ni, (n0, N) in enumerate(nts):
                sc.op("dve", lambda e, o=lnv[:, n0:n0 + N], i=rstd[:, n0:n0 + N]:
                      e.tensor_tensor(out=o, in0=i, in1=i, op=ALU.mult),
                      reads=[("rstd", ni), ("lnv", ni)], writes=[("lnv", ni)])
            si = sbase
            for hh in range(2):
                slots = {}
                for jj, ocl, ni in slab_plan(4, nts) if hh == 0 else [(jj, ocl, ni) for jj in range(4)
                                                                      for ocl in range(4) for ni in range(len(nts))]:
                    if jj not in slots:
                        for k_ in range(len(slots), jj + 1):
                            slots[k_] = next_slab(si, hold=1 if (k_ == 1 and len(nts) > 1) else 0)
                            si += 1
                    slot = slots[jj]
                    sl = slab8[slot]
                    n0, N = nts[ni]
                    hc = jj * 4 + ocl
                    b = acc()
                    for kc in range(8):
                        mm(ps[:, b, :N], sl[:, kc, ocl * 128:(ocl + 1) * 128], hnv[:, kc, n0:n0 + N],
                           kc == 0, kc == 7, reads=[("slab", slot), ("hn", kc, ni)], writes=[("ps", b)])
                    ri = st["rt"]
                    st["rt"] ^= 1
                    sc.op("act", lambda e, o=rtmp[ri][:, :N], i=ps[:, b, :N]:
                          e.activation(out=o, in_=i, func=AF.Relu),
                          reads=[("ps", b)], writes=[("rt", ri)])
                    sc.op("act", lambda e, o=rtmp[ri][:, :N]: e.activation(out=o, in_=o, func=AF.Square),
                          reads=[("rt", ri)], writes=[("rt", ri)])
                    sc.op(alt(), lambda e, o=hidv[:, hc, n0:n0 + N], i=rtmp[ri][:, :N], r2=lnv[:, n0:n0 + N]:
                          e.tensor_tensor(out=o, in0=i, in1=r2, op=ALU.mult),
                          reads=[("rt", ri), ("lnv", ni)], writes=[("hid", hc, ni)])
                for j in range(4):
                    slot = next_slab(si)
                    si += 1
                    sl = slab16[slot]
                    for ocl in range(2):
                        oc = j * 2 + ocl
                        for ni, (n0, N) in enumerate(nts):
                            b = acc()
                            for kc in range(16):
                                mm(ps[:, b, :N], sl[:, kc, ocl * 128:(ocl + 1) * 128], hidv[:, kc, n0:n0 + N],
                                   kc == 0, kc == 15, reads=[("slab", slot), ("hid", kc, ni)], writes=[("ps", b)])
                            if hh == 0:
                                sc.op("act", lambda e, o=mx[:, oc, n0:n0 + N], i=ps[:, b, :N]:
                                      e.activation(out=o, in_=i, func=AF.Copy),
                                      reads=[("ps", b)], writes=[(mkey, oc, ni)])
                            else:
                                sc.op("dve", lambda e, o=mx[:, oc, n0:n0 + N], i=ps[:, b, :N]:
                                      e.tensor_tensor(out=o, in0=i, in1=o, op=ALU.add),
                                      reads=[("ps", b), (mkey, oc, ni)], writes=[(mkey, oc, ni)])
                                post_stats_add(mx[:, oc, n0:n0 + N], [(mkey, oc, ni)], oc, ni, N)
            mk = lambda c, ni: [(mkey, c, ni)]
            post_stats_finish(nts)
            apply_post(l * 4 + 3, mx, mk, res, rkeys, T, nts)

        def conv_layer(l, T, nts):
            nn = len(nts)
            sbase = 24 * l
            hk = lambda c, ni: [("h", c, ni)]
            barrier()
            cast_pre(h, hk, hn, "hn", T, nts)
            for ni, (n0, N) in enumerate(nts):
                sc.op("dve", lambda e, o=lnv[:, n0:n0 + N], i=rstd[:, n0:n0 + N]:
                      e.tensor_tensor(out=o, in0=i, in1=i, op=ALU.mult),
                      reads=[("rstd", ni), ("lnv", ni)], writes=[("lnv", ni)])
            si = sbase
            slots = {}
            done = {}
            for sj, ocl, ni in slab_plan(6, nts):
                if True:
                    if sj not in slots:
                        for k_ in range(len(slots), sj + 1):
                            slots[k_] = next_slab(si, hold=1 if (k_ == 1 and len(nts) > 1) else 0)
                            si += 1
                    slot = slots[sj]
                    sl = slab8[slot]
                    n0, N = nts[ni]
                    oc = INPERM[sj] * 4 + ocl
                    f = oc % 8
                    j = f % 2
                    b = acc()
                    for kc in range(8):
                        mm(ps[:, b, :N], sl[:, kc, ocl * 128:(ocl + 1) * 128], hn[:, kc, n0:n0 + N],
                           kc == 0, kc == 7, reads=[("slab", slot), ("hn", kc, ni)], writes=[("ps", b)])
                    if oc < 8:
                        sc.op("dve", lambda e, o=bgate[:, f, n0:n0 + N], i=ps[:, b, :N], r=rstd[:, n0:n0 + N]:
                              e.tensor_tensor(out=o, in0=i, in1=r, op=ALU.mult),
                              reads=[("ps", b), ("rstd", ni)], writes=[("bg", f, ni)])
                    elif oc < 16:
                        sc.op("dve", lambda e, o=cy[:, f, n0:n0 + N], i=ps[:, b, :N], r=lnv[:, n0:n0 + N]:
                              e.tensor_tensor(out=o, in0=i, in1=r, op=ALU.mult),
                              reads=[("ps", b), ("lnv", ni)], writes=[("cy", f, ni)])
                    else:
                        sc.op("dve", lambda e, o=ub[j][:, 2 + n0:2 + n0 + N], i=ps[:, b, :N],
                              c_=cy[:, f, n0:n0 + N]: e.tensor_tensor(out=o, in0=i, in1=c_, op=ALU.mult),
                              reads=[("ps", b), ("cy", f, ni)], writes=[("ub", j, ni)])
                    done[oc] = done.get(oc, 0) + 1
                    if oc >= 16 and done[oc] == nn:
                        ubk = [("ub", j, n_) for n_ in range(nn)]
                        sc.op("pool", lambda e, o=ub[j][:, 0:2], i=ust3l[l][:, f, :]: e.tensor_copy(out=o, in_=i),
                              reads=[("ust", l, f)], writes=[("ubs", j)])
                        cwi = lambda tap: convw_sb[:, (l * 3 + tap) * 8 + f:(l * 3 + tap) * 8 + f + 1]
                        sc.op("act", lambda e, o=ctmp[j][:, :T], i=ub[j][:, 0:T], s_=cwi(0):
                              e.activation(out=o, in_=i, func=AF.Copy, scale=s_),
                              reads=ubk + [("ubs", j), "convw"], writes=[("ct", j)])
                        for tap in (1, 2):
                            sc.op("dve", lambda e, o=ctmp[j][:, :T], i=ub[j][:, tap:tap + T], s_=cwi(tap):
                                  e.scalar_tensor_tensor(out=o, in0=i, scalar=s_, in1=o, op0=ALU.mult, op1=ALU.add),
                                  reads=ubk + [("ubs", j), ("ct", j), "convw"], writes=[("ct", j)])
                        sc.op("pool" if f % 2 == 0 else "dve",
                              lambda e, o=cy[:, f, :T], i=ctmp[j][:, :T], g=bgate[:, f, :T]:
                              e.tensor_tensor(out=o, in0=i, in1=g, op=ALU.mult),
                              reads=[("ct", j)] + [("bg", f, n_) for n_ in range(nn)],
                              writes=[("cy", f, n_) for n_ in range(nn)])
                        sc.op("pool", lambda e, o=ust3l[l][:, f, :], i=ub[j][:, T:T + 2]: e.tensor_copy(out=o, in_=i),
                              reads=ubk + [("ubs", j)], writes=[("ust", l, f)])
            for sj in range(2):
                slot = next_slab(si)
                si += 1
                sl = slab8[slot]
                for ocl in range(4):
                    oc = sj * 4 + ocl
                    for ni, (n0, N) in enumerate(nts):
                        b = acc()
                        for kc in range(8):
                            mm(ps[:, b, :N], sl[:, kc, ocl * 128:(ocl + 1) * 128], cy[:, kc, n0:n0 + N],
                               kc == 0, kc == 7, reads=[("slab", slot), ("cy", kc, ni)], writes=[("ps", b)])
                        sc.op("act", lambda e, o=mix[:, oc, n0:n0 + N], i=ps[:, b, :N]:
                              e.activation(out=o, in_=i, func=AF.Copy),
                              reads=[("ps", b)], writes=[("mix", oc, ni)])
                        post_stats_add(ps[:, b, :N], [("ps", b)], oc, ni, N)
            mk = lambda c, ni: [("mix", c, ni)]
            post_stats_finish(nts)
            apply_post(l * 4 + 1, mix, mk, h, hk, T, nts)
            mlp(l, sbase + 8, h, hk, mix, "mix", hn, hid, T, nts)

        def kv_phase(T, nts, kcol0, kb0, meta):
            hk = lambda c, ni: [("h", c, ni)]
            barrier()
            norm_stats(h, hk, T, nts)
            apply_pre(16, h, hk, hn, "hn", T, nts)
            si = 48
            for sj in range(2):
                slot = next_slab(si)
                si += 1
                sl = slab8[slot]
                for ocl, ni in order(sj == 0, nts):
                    n0, N = nts[ni]
                    oc = sj * 4 + ocl
                    b = acc()
                    for kc in range(8):
                        mm(ps[:, b, :N], sl[:, kc, ocl * 128:(ocl + 1) * 128], hn[:, kc, n0:n0 + N],
                           kc == 0, kc == 7, reads=[("slab", slot), ("hn", kc, ni)], writes=[("ps", b)])
                    ki = st["kt"]
                    st["kt"] ^= 1
                    sc.op("act" if ni == 0 else "dve",
                          (lambda e, o=ktmp[ki][:, :N], i=ps[:, b, :N]: e.activation(out=o, in_=i, func=AF.Copy))
                          if ni == 0 else
                          (lambda e, o=ktmp[ki][:, :N], i=ps[:, b, :N]: e.tensor_copy(out=o, in_=i)),
                          reads=[("ps", b)], writes=[("kt", ki)])
                    key = ("kTd", oc, kcol0 + n0)
                    sc.dma("pool", "kvst", kTd[oc, :, kcol0 + n0:kcol0 + n0 + N], ktmp[ki][:, :N],
                           reads=[("kt", ki)], writes=[key])
                    kT_keys[oc].append(key)
            for sj in range(2):
                slot = next_slab(si)
                si += 1
                sl = slab8[slot]
                nblk = 1 if meta else T // 128
                for jb in range(nblk):
                    M = NMETA if meta else 128
                    t0 = jb * 128
                    ni = t0 // 512
                    p0 = 112 if meta else 0
                    kb = kb0 + jb
                    b = acc()
                    for kc in range(8):
                        mm(ps[:M, b, :], hn[:, kc, t0:t0 + M], sl[:, kc, :], kc == 0, kc == 7,
                           reads=[("slab", slot), ("hn", kc, ni)], writes=[("ps", b)])
                    vi = st["vt"]
                    st["vt"] ^= 1
                    sc.op("act" if jb % 2 == 0 else "dve",
                          (lambda e, o=vtmp[vi][:M, :], i=ps[:M, b, :]: e.activation(out=o, in_=i, func=AF.Copy))
                          if jb % 2 == 0 else
                          (lambda e, o=vtmp[vi][:M, :], i=ps[:M, b, :]: e.tensor_copy(out=o, in_=i)),
                          reads=[("ps", b)], writes=[("vt", vi)])
                    dst = vd[sj * 4:(sj + 1) * 4, p0:p0 + M, kb, :].rearrange("c p f -> p c f")
                    srcv = vtmp[vi][:M, :].rearrange("p (c f) -> p c f", c=4)
                    keys = [("vd", sj * 4 + cc, kb) for cc in range(4)]
                    sc.dma("pool", "kvst", dst, srcv, reads=[("vt", vi)], writes=keys)
                    for cc in range(4):
                        v_keys[sj * 4 + cc].append(keys[cc])

        def kv_load(S, c, which):
            nkb = 8 * S + 9
            h0 = nkb // 2
            a_, bnd = (h0, nkb) if which == "hi" else (0, h0)
            sc.dma("pool", "kvld", kTs[:, a_ * 128:bnd * 128], kTd[c, :, a_ * 128:bnd * 128],
                   reads=list(kT_keys[c]), writes=[("kTs", which)])
            sc.dma("pool", "kvld", vs[:, a_:bnd, :], vd[c, :, a_:bnd, :],
                   reads=list(v_keys[c]), writes=[("vs", which)])

        def attention(S):
            nkb = 8 * S + 9
            h0 = nkb // 2
            tiles = list(range(nkb - 1, -1, -1))
            n = nkb
            seq = [(c, t) for c in range(8) for t in range(n)]
            G = len(seq)

            def half(kb):
                return "hi" if kb >= h0 else "lo"

            def issue_load(c, which):
                kv_load(S, c, which)

            def cs_of(t):
                rel = tiles[t] - (8 * S + 1)
                if rel < 0:
                    return 0, rel
                return 128 * (rel // 2), rel

            def S1(g):
                c, t = seq[g]
                kb = tiles[t]
                cs, rel = cs_of(t)
                for hh in range(2):
                    pr = slice(64 * hh, 64 * hh + 64)
                    mm(ps[:, hh, cs:512], kTs[pr, kb * 128:(kb + 1) * 128], qT[pr, c, cs:512], True, rel < 0,
                       reads=[("kTs", half(kb)), ("qT", c)], writes=[("ps", hh)])
                if rel >= 0:
                    for hh in range(2):
                        mm(ps[:, hh, cs:512], ident_sb[:, :], dmask3[:, rel, cs:512], False, True,
                           reads=["ident", "dmask"], writes=[("ps", hh)])

            def S2(g):
                c, t = seq[g]
                cs, rel = cs_of(t)
                b = g % 2
                sc.op("act", lambda e, o=e2[b][:, :, cs:512], i=ps[:, 0:2, cs:512]:
                      e.activation(out=o, in_=i, func=AF.Exp),
                      reads=[("ps", 0), ("ps", 1)], writes=[("e2", b)])

            def S3(g):
                c, t = seq[g]
                cs, rel = cs_of(t)
                b = g % 2
                sc.op("act", lambda e, o=sp2[b][:, :, cs:512], i=e2[b][:, :, cs:512]:
                      e.activation(out=o, in_=i, func=AF.Ln, bias=one_ap, scale=1.0),
                      reads=[("e2", b), "cst1"], writes=[("sp2", b)])

            def S4(g):
                c, t = seq[g]
                kb = tiles[t]
                cs, rel = cs_of(t)
                b = g % 2
                for hh in range(2):
                    pr = slice(64 * hh, 64 * hh + 64)
                    mm(ps[:, 2 + hh, cs:512], kTs[pr, kb * 128:(kb + 1) * 128], qT[pr, c, cs:512], True, False,
                       reads=[("kTs", half(kb)), ("qT", c)], writes=[("ps", 2 + hh)])
                for hh in range(2):
                    mm(ps[:, 2 + hh, cs:512], negT_sb[:, :], sp2[b][:, hh, cs:512], False, t == 0 and rel < 0,
                       reads=[("sp2", b), "negT"], writes=[("ps", 2 + hh)])
                if t > 0:
                    for hh in range(2):
                        mm(ps[:, 2 + hh, cs:512], negones_sb[:, :], Lsum[:, hh, cs:512], False, rel < 0,
                           reads=[("Lsum",), "negones"], writes=[("ps", 2 + hh)])
                if rel >= 0:
                    for hh in range(2):
                        mm(ps[:, 2 + hh, cs:512], ident_sb[:, :], dmask3[:, rel, cs:512], False, True,
                           reads=["ident", "dmask"], writes=[("ps", 2 + hh)])

            def UPD(g):
                c, t = seq[g]
                cs, rel = cs_of(t)
                b = g % 2
                if t == n - 1:
                    return
                if t == 0:
                    sc.op("pool", lambda e, o=Lsum[:, :, :]: e.memset(o, 0.0), writes=[("Lsum",)])
                sc.op("dve", lambda e, o=Lsum[:, :, cs:512], i=sp2[b][:, :, cs:512]:
                      e.tensor_tensor(out=o, in0=o, in1=i, op=ALU.add),
                      reads=[("sp2", b), ("Lsum",)], writes=[("Lsum",)])

            def S6(g):
                c, t = seq[g]
                cs, rel = cs_of(t)
                b = g % 2
                sc.op("act", lambda e, o=A2[b][:, :, cs:512], i=ps[:, 2:4, cs:512]:
                      e.activation(out=o, in_=i, func=AF.Exp),
                      reads=[("ps", 2), ("ps", 3)], writes=[("A2", b)])

            def S7(g):
                c, t = seq[g]
                kb = tiles[t]
                cs, rel = cs_of(t)
                b = g % 2
                Ob = 4 + 2 * (c % 2)
                if t == 0:
                    for hh in range(2):
                        mm(ps[:, Ob + hh, :], ztile[:, :], dmask3[:, 0, :], True, False,
                           reads=["ztile", "dmask"], writes=[("ps", Ob + hh)])
                for hh in range(2):
                    mm(ps[:, Ob + hh, cs:512], vs[:, kb, :], A2[b][:, hh, cs:512], False, t == n - 1,
                       reads=[("vs", half(kb)), ("A2", b)], writes=[("ps", Ob + hh)])
                if kb == h0 and c + 1 < 8:
                    issue_load(c + 1, "hi")
                if t == n - 1:
                    if c + 1 < 8:
                        issue_load(c + 1, "lo")
                    for hh in range(2):
                        pr = slice(64 * hh, 64 * hh + 64)
                        sc.op("dve", lambda e, o=oT[pr, c, :], i=ps[pr, Ob + hh, :]: e.tensor_copy(out=o, in_=i),
                              reads=[("ps", Ob + hh)], writes=[("oT", c, hh)])

            S1(0)
            S2(0)
            if G > 1:
                S1(1)
            for g in range(G + 1):
                if g + 1 < G:
                    S2(g + 1)
                if g < G:
                    S3(g)
                if g >= 1:
                    S6(g - 1)
                if g + 2 < G:
                    S1(g + 2)
                if g < G:
                    S4(g)
                    UPD(g)
                if g >= 1:
                    S7(g - 1)

        def attn_layer(S, lb):
            l = 2 + lb
            sbase = 52 + 20 * lb
            T = 512
            nts = [(0, 512)]
            hok = lambda c, ni: [("ho", c)]
            barrier()
            kv_load(S, 0, "hi")
            kv_load(S, 0, "lo")
            cast_pre(hown, hok, hnB, "hn", T, nts)
            si = sbase
            for sj in range(2):
                slot = next_slab(si)
                si += 1
                sl = slab8[slot]
                for ocl in range(4):
                    oc = sj * 4 + ocl
                    b = acc()
                    for kc in range(8):
                        mm(ps[:, b, :], sl[:, kc, ocl * 128:(ocl + 1) * 128], hnB[:, kc, :], kc == 0, kc == 7,
                           reads=[("slab", slot), ("hn", kc, 0)], writes=[("ps", b)])
                    sc.op("dve", lambda e, o=qT[:, oc, :], i=ps[:, b, :]:
                          e.scalar_tensor_tensor(out=o, in0=i, scalar=0.125, in1=rstd[:, 0:512],
                                                 op0=ALU.mult, op1=ALU.mult),
                          reads=[("ps", b), ("rstd", 0)], writes=[("qT", oc)])
            attention(S)
            for sj in range(2):
                slot = next_slab(si)
                si += 1
                sl = slab8[slot]
                for ocl in range(4):
                    oc = sj * 4 + ocl
                    b = acc()
                    for kc in range(8):
                        mm(ps[:, b, :], sl[:, kc, ocl * 128:(ocl + 1) * 128], oT[:, kc, :], kc == 0, kc == 7,
                           reads=[("slab", slot), ("oT", kc, 0), ("oT", kc, 1)], writes=[("ps", b)])
                    sc.op("act", lambda e, o=mixB[:, oc, :], i=ps[:, b, :]:
                          e.activation(out=o, in_=i, func=AF.Copy),
                          reads=[("ps", b)], writes=[("mixB", oc, 0)])
                    post_stats_add(ps[:, b, :], [("ps", b)], oc, 0, 512)
            mk = lambda c, ni: [("mixB", c, 0)]
            post_stats_finish(nts)
            apply_post(l * 4 + 1, mixB, mk, hown, hok, T, nts)
            mlp(l, sbase + 4, hown, hok, mixB, "mixB", hnB, hidB, T, nts)

        def select_own():
            for c in range(8):
                hv = h[:, c, :].rearrange("p (i two q) -> p i two q", two=2, q=128)
                ev = hv[:, :, 0, :]
                od = hv[:, :, 1, :]
                dst = hown[:, c, :].rearrange("p (i q) -> p i q", q=128)
                j = st["tc"]
                st["tc"] ^= 1
                tmp = tmpc[j][:, 0:512].rearrange("p (i q) -> p i q", q=128)
                sc.op("dve", lambda e, o=tmp, i=ev: e.tensor_scalar(out=o, in0=i, scalar1=pid_sb[:, 0:1], scalar2=None,
                                                                op0=ALU.mult),
                      reads=[("h", c, 0), ("h", c, 1), "pid"], writes=[("tc", j)])
                sc.op("dve", lambda e, o=dst, i=od, t_=tmp:
                      e.scalar_tensor_tensor(out=o, in0=i, scalar=pid_sb[:, 1:2], in1=t_, op0=ALU.mult, op1=ALU.add),
                      reads=[("h", c, 0), ("h", c, 1), ("tc", j), "pid"], writes=[("ho", c)])

        sc.dma("sp", "x", h[:, :, 0:NMETA], metaT.rearrange("(c p) t -> p c t", p=128),
               writes=[("h", c, 0) for c in range(8)])
        mnts = [(0, NMETA)]
        conv_layer(0, NMETA, mnts)
        conv_layer(1, NMETA, mnts)
        kv_phase(NMETA, mnts, 112, 0, True)
        nts = [(0, 512), (512, 512)]
        for S in range(nsg):
            barrier()
            for ni in range(2):
                sc.dma("sp", "x", h[:, :, ni * 512:(ni + 1) * 512],
                       xT[:, S * 1024 + ni * 512:S * 1024 + (ni + 1) * 512].rearrange("(c p) t -> p c t", p=128),
                       writes=[("h", c, ni) for c in range(8)])
            conv_layer(0, 1024, nts)
            conv_layer(1, 1024, nts)
            kv_phase(1024, nts, 128 + S * 1024, 1 + 8 * S, False)
            barrier()
            select_own()
            barrier()
            attn_layer(S, 0)
            attn_layer(S, 1)
            sc.dma("pool", "out", outT[:, S * 512:(S + 1) * 512].rearrange("(c p) t -> p c t", p=128), hown[:, :, :],
                   reads=[("ho", c) for c in range(8)], writes=[("out", S)])
        barrier()
        assert st["slab_k"] == len(slab_seq), (st["slab_k"], len(slab_seq))

        with nc.Block() as block:
            @block.tensor
            def _(e):
                for f in sc.ops["pe"]:
                    f(e)

            @block.scalar
            def _(e):
                for f in sc.ops["act"]:
                    f(e)

            @block.vector
            def _(e):
                for f in sc.ops["dve"]:
                    f(e)

            @block.gpsimd
            def _(e):
                for f in sc.ops["pool"]:
                    f(e)

            @block.sync
            def _(e):
                for f in sc.ops["sp"]:
                    f(e)
    return nc


def _host_consts(p):
    bf = ml_dtypes.bfloat16
    j = np.arange(128)[:, None]
    s = np.arange(128)[None, :]
    negT = np.where(j >= s, -1.0, 0.0).astype(bf)
    negones = np.full((128, 128), -1.0, dtype=bf)
    ones = np.ones((128, 128), dtype=bf)
    dm = np.zeros((128, 8, 4, 128), dtype=np.float32)
    tri = (np.arange(128)[:, None] < np.arange(128)[None, :]).astype(np.float32)
    for rel in range(8):
        for i in range(4):
            qb = 2 * i + p
            if rel < qb:
                dm[:, rel, i, :] = 1.0
            elif rel == qb:
                dm[:, rel, i, :] = tri
    dmask = ((dm - 1.0) * 30000.0).reshape(128, 8 * 512).astype(bf)
    ident = np.eye(128, dtype=np.float32).astype(bf)
    pid = np.zeros((128, 2), dtype=np.float32)
    pid[:, 0] = 1.0 - p
    pid[:, 1] = float(p)
    return negT, negones, ones, dmask, pid, ident


def kernel(x, meta_tokens, norm_gains, conv_in_proj, conv_w, conv_out_proj, kv_norm,
           w_k, w_v, w_q, w_o, mlp_w1, mlp_w2):
    f32 = np.float32
    x = np.asarray(x, dtype=f32)
    B, SEQ, _ = x.shape
    nsg = SEQ // 1024
    assert B * 2 == 8 and SEQ == nsg * 1024
    nc = build(nsg)

    def pc(v):
        return np.asarray(v, dtype=f32).reshape(8, 128).T

    ng = np.asarray(norm_gains, dtype=f32)
    gl = [pc(ng[l, j]) for l in range(4) for j in range(4)] + [pc(np.asarray(kv_norm, dtype=f32))]
    gains = np.ascontiguousarray(np.concatenate(gl, axis=1))
    cw = np.asarray(conv_w, dtype=f32)
    convw = np.ascontiguousarray(np.concatenate([pc(cw[l, t]) for l in range(2) for t in range(3)], axis=1))
    metaT = np.ascontiguousarray(np.asarray(meta_tokens, dtype=f32).T)
    shared = {
        "metaT": metaT, "gains": gains, "convw": convw,
        "conv_in_proj": np.ascontiguousarray(conv_in_proj, dtype=f32),
        "conv_out_proj": np.ascontiguousarray(conv_out_proj, dtype=f32),
        "w_k": np.ascontiguousarray(w_k, dtype=f32), "w_v": np.ascontiguousarray(w_v, dtype=f32),
        "w_q": np.ascontiguousarray(w_q, dtype=f32), "w_o": np.ascontiguousarray(w_o, dtype=f32),
        "mlp_w1": np.ascontiguousarray(mlp_w1, dtype=f32), "mlp_w2": np.ascontiguousarray(mlp_w2, dtype=f32),
    }
    consts = [_host_consts(p) for p in range(2)]
    xTs = [np.ascontiguousarray(x[b].T) for b in range(B)]
    in_maps = []
    for core in range(8):
        b, p = core // 2, core % 2
        negT, negones, ones, dmask, pid, ident = consts[p]
        m = dict(shared)
        m.update({"xT": xTs[b], "pid": pid, "dmask": dmask, "negT": negT, "negones": negones, "ones": ones,
                  "ident": ident})
        in_maps.append(m)
    res = run_bass_kernel_spmd(nc, in_maps, core_ids=list(range(8)))
    out = np.empty((B, SEQ, D), dtype=f32)
    for core in range(8):
        b, p = core // 2, core % 2
        oT = np.asarray(res.results[core]["outT"], dtype=f32)
        o = oT.T.reshape(nsg, 4, 128, D)
        ov = out[b].reshape(nsg, 4, 2, 128, D)
        ov[:, :, p] = o
    return out
```

```python
import numpy as np
import ml_dtypes
from contextlib import ExitStack
import concourse.bass as bass
import concourse.mybir as mybir
from concourse.bass_utils import run_bass_kernel_spmd

F32 = mybir.dt.float32
BF16 = mybir.dt.bfloat16
AF = mybir.ActivationFunctionType
ALU = mybir.AluOpType

D = 1024
NMETA = 16
NSLOT = 4
ENGS = ["pe", "act", "dve", "pool", "sp"]


class _Sem:
    def __init__(self, h, i):
        self.h = h
        self.i = i


class Sched:
    def __init__(self, new_sem):
        self.new_sem = new_sem
        self.ops = {e: [] for e in ENGS}
        self.cnt = {e: 0 for e in ENGS}
        self.prog = {e: new_sem("prog_" + e) for e in ENGS}
        self.waited = {e: {} for e in ENGS}
        self.lastw = {}
        self.readers = {}
        self.pools = {}
        self.n_wait = 0

    def add_pool(self, name, n):
        self.pools[name] = {"sems": [[self.new_sem("%s%d" % (name, i)), 0] for i in range(n)], "next": 0}

    def _need(self, E, tok):
        sem, val, eng = tok
        if eng == E and E == "pe":
            return
        w = self.waited[E]
        if w.get(sem.i, 0) >= val:
            return
        w[sem.i] = val
        h = sem.h
        self.n_wait += 1
        self.ops[E].append(lambda e, h=h, val=val: e.wait_ge(h, val))

    def _deps(self, E, reads, writes):
        for k in reads:
            t = self.lastw.get(k)
            if t is not None:
                self._need(E, t)
        for k in writes:
            t = self.lastw.get(k)
            if t is not None:
                self._need(E, t)
            r = self.readers.get(k)
            if r:
                for t in r.values():
                    self._need(E, t)

    def _commit(self, tok, reads, writes):
        for k in writes:
            self.lastw[k] = tok
            self.readers[k] = {}
        for k in reads:
            d = self.readers.setdefault(k, {})
            old = d.get(tok[0].i)
            if old is None or old[1] < tok[1]:
                d[tok[0].i] = tok

    def op(self, E, fn, reads=(), writes=()):
        self._deps(E, reads, writes)
        self.cnt[E] += 1
        sem = self.prog[E]
        h = sem.h
        self.ops[E].append(lambda e, fn=fn, h=h: fn(e).then_inc(h, 1))
        tok = (sem, self.cnt[E], E)
        self._commit(tok, reads, writes)
        return tok

    def dma(self, Q, pool, out_ap, in_ap, reads=(), writes=()):
        self._deps(Q, reads, writes)
        p = self.pools[pool]
        ent = p["sems"][p["next"] % len(p["sems"])]
        p["next"] += 1
        sem = ent[0]
        if ent[1] > 0:
            self._need(Q, (sem, 16 * ent[1], None))
        ent[1] += 1
        h = sem.h
        self.ops[Q].append(lambda e, o=out_ap, i=in_ap, h=h: e.dma_start(out=o, in_=i).then_inc(h, 16))
        tok = (sem, 16 * ent[1], None)
        self._commit(tok, reads, writes)
        return tok

    def barrier(self, dummy):
        M = "pool"
        for E in ENGS:
            if E != M and self.cnt[E] > 0:
                self._need(M, (self.prog[E], self.cnt[E], E))
        for p in self.pools.values():
            for ent in p["sems"]:
                if ent[1] > 0:
                    self._need(M, (ent[0], 16 * ent[1], None))
        prev = self.lastw.get("_bar")
        if prev is not None:
            self._need(M, prev)
        tok = self.op(M, lambda e: e.memset(dummy, 0.0))
        for E in ENGS:
            if E != M:
                self._need(E, tok)
        self.lastw.clear()
        self.readers.clear()
        self.lastw["_bar"] = tok


def build(nsg):
    nc = bass.Bass("TRN2", target_bir_lowering=False)
    TSEQ = nsg * 1024
    NKB = 8 * nsg + 1
    NK = NKB * 128

    def din(name, shape, dt=F32):
        return nc.dram_tensor(name, list(shape), dt, kind="ExternalInput").ap()

    xT = din("xT", [D, TSEQ])
    metaT = din("metaT", [D, NMETA])
    gains_d = din("gains", [128, 17 * 8])
    convw_d = din("convw", [128, 48])
    pid_d = din("pid", [128, 2])
    dmask_d = din("dmask", [128, 8 * 512], BF16)
    negT_d = din("negT", [128, 128], BF16)
    negones_d = din("negones", [128, 128], BF16)
    ones_d = din("ones", [128, 128], BF16)
    ident_d = din("ident", [128, 128], BF16)
    cin = din("conv_in_proj", [2, D, 3 * D])
    cout = din("conv_out_proj", [2, D, D])
    wk = din("w_k", [D, D])
    wv = din("w_v", [D, D])
    wq = din("w_q", [2, D, D])
    wo = din("w_o", [2, D, D])
    w1 = din("mlp_w1", [4, D, 4 * D])
    w2 = din("mlp_w2", [4, 4 * D, D])
    outT = nc.dram_tensor("outT", [D, nsg * 512], F32, kind="ExternalOutput").ap()

    wsl = nc.dram_tensor("wsl", [92, 128, 4096], BF16).ap()
    kTd = nc.dram_tensor("kTd", [8, 128, NK], BF16).ap()
    vd = nc.dram_tensor("vd", [8, 128, NKB, 128], BF16).ap()

    def k1024(W, j):
        return W.rearrange("(kc p) n -> p kc n", p=128)[:, :, j * 512:(j + 1) * 512]

    def w2slab(W, hh, j):
        return W.rearrange("(hh kc p) n -> hh p kc n", hh=2, p=128)[hh][:, :, j * 256:(j + 1) * 256]

    slab_src = []

    def add_mlp(l):
        for hh in range(2):
            for j in range(4):
                slab_src.append((k1024(w1[l], hh * 4 + j), 8))
            for j in range(4):
                slab_src.append((w2slab(w2[l], hh, j), 16))

    INPERM = [0, 1, 2, 4, 3, 5]
    for l in range(2):
        for j in range(6):
            slab_src.append((k1024(cin[l], INPERM[j]), 8))
        for j in range(2):
            slab_src.append((k1024(cout[l], j), 8))
        add_mlp(l)
    for j in range(2):
        slab_src.append((k1024(wk, j), 8))
    for j in range(2):
        slab_src.append((k1024(wv, j), 8))
    for lb in range(2):
        for j in range(2):
            slab_src.append((k1024(wq[lb], j), 8))
        for j in range(2):
            slab_src.append((k1024(wo[lb], j), 8))
        add_mlp(2 + lb)
    assert len(slab_src) == 92

    slab_seq = list(range(52))
    for S in range(nsg):
        slab_seq += list(range(92))

    with ExitStack() as es:
        def sb(name, shape, dt):
            return es.enter_context(nc.sbuf_tensor(name, list(shape), dt))

        sem_count = [0]

        def new_sem(name):
            h = es.enter_context(nc.semaphore(name))
            sem_count[0] += 1
            return _Sem(h, sem_count[0])

        slabs = [sb("slab%d" % i, [128, 4096], BF16) for i in range(NSLOT)]
        R1 = sb("R1", [128, 8192], F32)
        R2 = sb("R2", [128, 8192], F32)
        hn_t = sb("hn", [128, 8192], BF16)
        sqt = [sb("sq%d" % i, [128, 1024], BF16) for i in range(2)]
        lnv = sb("lnv", [128, 1024], F32)
        rstd = sb("rstd", [128, 1024], F32)
        tmpc = [sb("tmpc%d" % i, [128, 1024], F32) for i in range(2)]
        dmask_sb = sb("dmask_sb", [128, 4096], BF16)
        gains_sb = sb("gains_sb", [128, 17 * 8], F32)
        convw_sb = sb("convw_sb", [128, 48], F32)
        pid_sb = sb("pid_sb", [128, 2], F32)
        negT_sb = sb("negT_sb", [128, 128], BF16)
        negones_sb = sb("negones_sb", [128, 128], BF16)
        ones_sb = sb("ones_sb", [128, 128], BF16)
        ident_sb = sb("ident_sb", [128, 128], BF16)
        ztile = sb("ztile", [128, 128], BF16)
        cst = sb("cst", [128, 4], F32)
        ust = sb("ust", [128, 32], F32)
        dummy = sb("dummyt", [128, 4], F32)
        arena = sb("arena", [128, 16384], F32)
        ps = es.enter_context(nc.psum_tensor("ps", [128, 8, 512], F32))

        sc = Sched(new_sem)
        for i in range(NSLOT):
            sc.add_pool("slab%d" % i, 1)
        sc.add_pool("x", 2)
        sc.add_pool("const", 4)
        sc.add_pool("pl", 3)
        sc.add_pool("pst", 2)
        sc.add_pool("kvst", 8)
        sc.add_pool("kvld", 4)
        sc.add_pool("out", 2)

        def carve(off, n, dt):
            if dt == F32:
                return arena[:, off // 4: off // 4 + n]
            return arena[:, off // 4: off // 4 + n // 2].bitcast(BF16)

        h = R1[:, :].rearrange("p (c t) -> p c t", c=8)
        mix = R2[:, :].rearrange("p (c t) -> p c t", c=8)
        hn = hn_t[:, :].rearrange("p (c t) -> p c t", c=8)
        hown = R2[:, 0:4096].rearrange("p (c t) -> p c t", c=8)
        mixB = R2[:, 4096:8192].rearrange("p (c t) -> p c t", c=8)
        hnB = hn_t[:, 0:4096].rearrange("p (c t) -> p c t", c=8)
        qT = R1[:, 0:2048].bitcast(BF16).rearrange("p (c t) -> p c t", c=8)
        oT = R1[:, 2048:4096].bitcast(BF16).rearrange("p (c t) -> p c t", c=8)
        slab8 = [s[:, :].rearrange("p (k o) -> p k o", k=8) for s in slabs]
        slab16 = [s[:, :].rearrange("p (k o) -> p k o", k=16) for s in slabs]
        dmask3 = dmask_sb[:, :].rearrange("p (r t) -> p r t", r=8)
        ust3l = [ust[:, l * 16:(l + 1) * 16].rearrange("p (c t) -> p c t", c=8) for l in range(2)]
        eps_ap = cst[:, 0:1]
        one_ap = cst[:, 1:2]

        bgate = carve(0, 8192, BF16).rearrange("p (c t) -> p c t", c=8)
        cy = carve(16384, 8192, BF16).rearrange("p (c t) -> p c t", c=8)
        ub = [carve(32768 + i * 4112, 1028, F32) for i in range(2)]
        ctmp = [carve(41216 + i * 4096, 1024, F32) for i in range(2)]
        hid = carve(0, 16384, BF16).rearrange("p (c t) -> p c t", c=16)
        hidB = carve(0, 8192, BF16).rearrange("p (c t) -> p c t", c=16)
        rtmp = [carve(32768 + i * 2048, 512, F32) for i in range(2)]
        ktmp = [carve(i * 1024, 512, BF16) for i in range(2)]
        vtmp = [carve(2048 + i * 1024, 512, BF16) for i in range(2)]
        kTs = carve(0, 8320, BF16)
        vs = carve(16640, 8320, BF16).rearrange("p (k f) -> p k f", f=128)
        e2 = [carve(33280 + i * 4096, 1024, F32).rearrange("p (a t) -> p a t", a=2) for i in range(2)]
        sp2 = [carve(41472 + i * 2048, 1024, BF16).rearrange("p (a t) -> p a t", a=2) for i in range(2)]
        A2 = [carve(45568 + i * 2048, 1024, BF16).rearrange("p (a t) -> p a t", a=2) for i in range(2)]
        Lsum = carve(49664, 1024, BF16).rearrange("p (a t) -> p a t", a=2)
        stg = [carve(i * 16384, 4096, F32) for i in range(3)]
        cb = [carve(49152 + i * 8192, 4096, BF16) for i in range(2)]

        st = {"acc": 0, "alt": 0, "sq": 0, "slab_k": 0, "slab_loaded": 0, "rt": 0, "kt": 0, "vt": 0, "tc": 0}

        def acc():
            st["acc"] = (st["acc"] + 1) % 4
            return st["acc"]

        def alt():
            st["alt"] ^= 1
            return "dve" if st["alt"] else "pool"

        def barrier():
            sc.barrier(dummy[:, 0:1])

        for dst, src, key in [(gains_sb, gains_d, "gains"), (convw_sb, convw_d, "convw"), (pid_sb, pid_d, "pid"),
                              (dmask_sb, dmask_d, "dmask"), (negT_sb, negT_d, "negT"),
                              (negones_sb, negones_d, "negones"), (ones_sb, ones_d, "ones"),
                              (ident_sb, ident_d, "ident")]:
            sc.dma("sp", "const", dst[:, :], src, writes=[key])
        sc.op("pool", lambda e: e.memset(ztile[:, :], 0.0), writes=["ztile"])
        sc.op("pool", lambda e: e.memset(cst[:, 0:1], 1e-6), writes=["cst0"])
        sc.op("pool", lambda e: e.memset(cst[:, 1:2], 1.0), writes=["cst1"])
        sc.op("pool", lambda e: e.memset(ust[:, :], 0.0), writes=[("ust", l_, f) for l_ in range(2) for f in range(8)])
        kT_keys = [[] for _ in range(8)]
        v_keys = [[] for _ in range(8)]
        for c in range(8):
            sc.dma("pool", "kvst", kTd[c, :, 0:112], ztile[:, 0:112], reads=["ztile"], writes=[("kTd", c, "z")])
            kT_keys[c].append(("kTd", c, "z"))
            sc.dma("pool", "kvst", vd[c, 0:112, 0, :], ztile[0:112, :], reads=["ztile"], writes=[("vd", c, "z")])
            v_keys[c].append(("vd", c, "z"))

        fold = {}
        for l_ in range(4):
            base_ = (24 * l_ + 8) if l_ < 2 else (52 + 20 * (l_ - 2) + 4)
            for hh_ in range(2):
                for j_ in range(4):
                    fold[base_ + hh_ * 8 + j_] = l_ * 4 + 2
        for lb_ in range(2):
            for j_ in range(2):
                fold[52 + 20 * lb_ + j_] = (2 + lb_) * 4 + 0
        for s, (src, kc) in enumerate(slab_src):
            i3 = s % 3
            i2 = s % 2
            dst = stg[i3].rearrange("p (k o) -> p k o", k=kc)
            sc.dma("sp" if s % 2 == 0 else "act", "pl", dst, src, writes=[("stg", i3)])
            eng = "dve" if s % 2 == 0 else "pool"
            if s in fold:
                gi = fold[s]
                for kq in range(8):
                    sc.op("dve", lambda e, o=cb[i2][:, kq * 512:(kq + 1) * 512], i=stg[i3][:, kq * 512:(kq + 1) * 512],
                          g=gains_sb[:, gi * 8 + kq:gi * 8 + kq + 1]:
                          e.tensor_scalar(out=o, in0=i, scalar1=g, scalar2=None, op0=ALU.mult),
                          reads=[("stg", i3), "gains"], writes=[("cb", i2)])
            else:
                sc.op(eng, lambda e, o=cb[i2], i=stg[i3]: e.tensor_copy(out=o, in_=i),
                      reads=[("stg", i3)], writes=[("cb", i2)])
            sc.dma("pool", "pst", wsl[s], cb[i2], reads=[("cb", i2)], writes=[("wsl", s)])
        barrier()

        def issue_slab_loads(upto):
            while st["slab_loaded"] <= upto and st["slab_loaded"] < len(slab_seq):
                k = st["slab_loaded"]
                slot = k % NSLOT
                sc.dma("sp", "slab%d" % slot, slabs[slot][:, :], wsl[slab_seq[k]], writes=[("slab", slot)])
                st["slab_loaded"] += 1

        def next_slab(expect, hold=0):
            k = st["slab_k"]
            assert slab_seq[k] == expect, (k, slab_seq[k], expect)
            issue_slab_loads(k + NSLOT - 1 - hold)
            st["slab_k"] += 1
            return k % NSLOT

        def mm(out, lhsT, rhs, start, stop, reads, writes):
            sc.op("pe", lambda e: e.matmul(out, lhsT=lhsT, rhs=rhs, start=start, stop=stop),
                  reads=reads, writes=writes)

        def norm_stats(src, skeys, T, nts):
            for ni, (n0, N) in enumerate(nts):
                for c in range(8):
                    j = st["sq"]
                    st["sq"] ^= 1
                    sc.op("act", lambda e, o=sqt[j][:, :N], i=src[:, c, n0:n0 + N]:
                          e.activation(out=o, in_=i, func=AF.Square),
                          reads=skeys(c, ni), writes=[("sq", j)])
                    mm(ps[:, 4 + ni, :N], ones_sb[:, :], sqt[j][:, :N], c == 0, c == 7,
                       reads=[("sq", j), "ones"], writes=[("ps", 4 + ni)])
                sc.op("act", lambda e, o=lnv[:, n0:n0 + N], i=ps[:, 4 + ni, :N]:
                      e.activation(out=o, in_=i, func=AF.Ln, bias=eps_ap, scale=1.0 / D),
                      reads=[("ps", 4 + ni), "cst0"], writes=[("lnv", ni)])
                sc.op("act", lambda e, o=rstd[:, n0:n0 + N], i=lnv[:, n0:n0 + N]:
                      e.activation(out=o, in_=i, func=AF.Exp, scale=-0.5),
                      reads=[("lnv", ni)], writes=[("rstd", ni)])

        def apply_pre(gi, src, skeys, dst, dkey, T, nts):
            for ni, (n0, N) in enumerate(nts):
                for c in range(8):
                    g = gains_sb[:, gi * 8 + c: gi * 8 + c + 1]
                    sc.op("dve", lambda e, o=dst[:, c, n0:n0 + N], i=src[:, c, n0:n0 + N], g=g,
                          r=rstd[:, n0:n0 + N]:
                          e.scalar_tensor_tensor(out=o, in0=i, scalar=g, in1=r, op0=ALU.mult, op1=ALU.mult),
                          reads=skeys(c, ni) + [("rstd", ni), "gains"], writes=[(dkey, c, ni)])

        def apply_post(gi, src, skeys, res, rkeys, T, nts):
            for ni, (n0, N) in enumerate(nts):
                for c in range(8):
                    g = gains_sb[:, gi * 8 + c: gi * 8 + c + 1]
                    j = st["tc"]
                    st["tc"] ^= 1
                    sc.op("dve", lambda e, o=tmpc[j][:, :N], i=src[:, c, n0:n0 + N], g=g, r=rstd[:, n0:n0 + N]:
                          e.scalar_tensor_tensor(out=o, in0=i, scalar=g, in1=r, op0=ALU.mult, op1=ALU.mult),
                          reads=skeys(c, ni) + [("rstd", ni), "gains"], writes=[("tc", j)])
                    sc.op("pool" if c % 2 == 0 else "dve", lambda e, o=res[:, c, n0:n0 + N], i=tmpc[j][:, :N]:
                          e.tensor_tensor(out=o, in0=o, in1=i, op=ALU.add),
                          reads=[("tc", j)] + rkeys(c, ni), writes=rkeys(c, ni))

        def cast_pre(src, skeys, dst, dkey, T, nts):
            for ni, (n0, N) in enumerate(nts):
                for c in range(8):
                    sc.op("act", lambda e, o=dst[:, c, n0:n0 + N], i=src[:, c, n0:n0 + N]:
                          e.activation(out=o, in_=i, func=AF.Copy),
                          reads=skeys(c, ni), writes=[(dkey, c, ni)])
            norm_stats(src, skeys, T, nts)

        def order(first, nts):
            if first and len(nts) > 1:
                return [(ocl, ni) for ni in range(len(nts)) for ocl in range(4)]
            return [(ocl, ni) for ocl in range(4) for ni in range(len(nts))]

        def slab_plan(nslabs, nts):
            plan = []
            j0 = 0
            if len(nts) > 1 and nslabs >= 2:
                for ni in range(len(nts)):
                    for jj in range(2):
                        for ocl in range(4):
                            plan.append((jj, ocl, ni))
                j0 = 2
            for jj in range(j0, nslabs):
                for ocl in range(4):
                    for ni in range(len(nts)):
                        plan.append((jj, ocl, ni))
            return plan

        pend = []

        def _emit_stat(item):
            j, oc, ni, N = item
            mm(ps[:, 4 + ni, :N], ones_sb[:, :], sqt[j][:, :N], oc == 0, oc == 7,
               reads=[("sq", j), "ones"], writes=[("ps", 4 + ni)])

        def post_stats_add(src_ap, src_keys, oc, ni, N):
            j = st["sq"]
            st["sq"] ^= 1
            sc.op("act", lambda e, o=sqt[j][:, :N], i=src_ap: e.activation(out=o, in_=i, func=AF.Square),
                  reads=src_keys, writes=[("sq", j)])
            pend.append((j, oc, ni, N))
            while len(pend) > 1:
                _emit_stat(pend.pop(0))

        def post_stats_finish(nts):
            while pend:
                _emit_stat(pend.pop(0))
            for ni, (n0, N) in enumerate(nts):
                sc.op("act", lambda e, o=lnv[:, n0:n0 + N], i=ps[:, 4 + ni, :N]:
                      e.activation(out=o, in_=i, func=AF.Ln, bias=eps_ap, scale=1.0 / D),
                      reads=[("ps", 4 + ni), "cst0"], writes=[("lnv", ni)])
                sc.op("act", lambda e, o=rstd[:, n0:n0 + N], i=lnv[:, n0:n0 + N]:
                      e.activation(out=o, in_=i, func=AF.Exp, scale=-0.5),
                      reads=[("lnv", ni)], writes=[("rstd", ni)])

        def mlp(l, sbase, res, rkeys, mx, mkey, hnv, hidv, T, nts):
            barrier()
            cast_pre(res, rkeys, hnv, "hn", T, nts)
            for ni, (n0, N) in enumerate(nts):
                sc.op("dve", lambda e, o=lnv[:, n0:n0 + N], i=rstd[:, n0:n0 + N]:
                      e.tensor_tensor(out=o, in0=i, in1=i, op=ALU.mult),
                      reads=[("rstd", ni), ("lnv", ni)], writes=[("lnv", ni)])
            si = sbase
            for hh in range(2):
                slots = {}
                for jj, ocl, ni in slab_plan(4, nts) if hh == 0 else [(jj, ocl, ni) for jj in range(4)
                                                                      for ocl in range(4) for ni in range(len(nts))]:
                    if jj not in slots:
                        for k_ in range(len(slots), jj + 1):
                            slots[k_] = next_slab(si, hold=1 if (k_ == 1 and len(nts) > 1) else 0)
                            si += 1
                    slot = slots[jj]
                    sl = slab8[slot]
                    n0, N = nts[ni]
                    hc = jj * 4 + ocl
                    b = acc()
                    for kc in range(8):
                        mm(ps[:, b, :N], sl[:, kc, ocl * 128:(ocl + 1) * 128], hnv[:, kc, n0:n0 + N],
                           kc == 0, kc == 7, reads=[("slab", slot), ("hn", kc, ni)], writes=[("ps", b)])
                    ri = st["rt"]
                    st["rt"] ^= 1
                    sc.op("act", lambda e, o=rtmp[ri][:, :N], i=ps[:, b, :N]:
                          e.activation(out=o, in_=i, func=AF.Relu),
                          reads=[("ps", b)], writes=[("rt", ri)])
                    sc.op("act", lambda e, o=rtmp[ri][:, :N]: e.activation(out=o, in_=o, func=AF.Square),
                          reads=[("rt", ri)], writes=[("rt", ri)])
                    sc.op(alt(), lambda e, o=hidv[:, hc, n0:n0 + N], i=rtmp[ri][:, :N], r2=lnv[:, n0:n0 + N]:
                          e.tensor_tensor(out=o, in0=i, in1=r2, op=ALU.mult),
                          reads=[("rt", ri), ("lnv", ni)], writes=[("hid", hc, ni)])
                for j in range(4):
                    slot = next_slab(si)
                    si += 1
                    sl = slab16[slot]
                    for ocl in range(2):
                        oc = j * 2 + ocl
                        for ni, (n0, N) in enumerate(nts):
                            b = acc()
                            for kc in range(16):
                                mm(ps[:, b, :N], sl[:, kc, ocl * 128:(ocl + 1) * 128], hidv[:, kc, n0:n0 + N],
                                   kc == 0, kc == 15, reads=[("slab", slot), ("hid", kc, ni)], writes=[("ps", b)])
                            if hh == 0:
                                sc.op("act", lambda e, o=mx[:, oc, n0:n0 + N], i=ps[:, b, :N]:
                                      e.activation(out=o, in_=i, func=AF.Copy),
                                      reads=[("ps", b)], writes=[(mkey, oc, ni)])
                            else:
                                sc.op("dve", lambda e, o=mx[:, oc, n0:n0 + N], i=ps[:, b, :N]:
                                      e.tensor_tensor(out=o, in0=i, in1=o, op=ALU.add),
                                      reads=[("ps", b), (mkey, oc, ni)], writes=[(mkey, oc, ni)])
                                post_stats_add(mx[:, oc, n0:n0 + N], [(mkey, oc, ni)], oc, ni, N)
            mk = lambda c, ni: [(mkey, c, ni)]
            post_stats_finish(nts)
            apply_post(l * 4 + 3, mx, mk, res, rkeys, T, nts)

        def conv_layer(l, T, nts, skip_barrier=False):
            nn = len(nts)
            sbase = 24 * l
            hk = lambda c, ni: [("h", c, ni)]
            if not skip_barrier:
                barrier()
            norm_stats(h, hk, T, nts)
            apply_pre(l * 4 + 0, h, hk, hn, "hn", T, nts)
            si = sbase
            slots = {}
            done = {}
            for sj, ocl, ni in slab_plan(6, nts):
                if True:
                    if sj not in slots:
                        for k_ in range(len(slots), sj + 1):
                            slots[k_] = next_slab(si, hold=1 if (k_ == 1 and len(nts) > 1) else 0)
                            si += 1
                    slot = slots[sj]
                    sl = slab8[slot]
                    n0, N = nts[ni]
                    oc = INPERM[sj] * 4 + ocl
                    f = oc % 8
                    j = f % 2
                    b = acc()
                    for kc in range(8):
                        mm(ps[:, b, :N], sl[:, kc, ocl * 128:(ocl + 1) * 128], hn[:, kc, n0:n0 + N],
                           kc == 0, kc == 7, reads=[("slab", slot), ("hn", kc, ni)], writes=[("ps", b)])
                    if oc < 8:
                        sc.op("act", lambda e, o=bgate[:, f, n0:n0 + N], i=ps[:, b, :N]:
                              e.activation(out=o, in_=i, func=AF.Copy),
                              reads=[("ps", b)], writes=[("bg", f, ni)])
                    elif oc < 16:
                        sc.op("act", lambda e, o=cy[:, f, n0:n0 + N], i=ps[:, b, :N]:
                              e.activation(out=o, in_=i, func=AF.Copy),
                              reads=[("ps", b)], writes=[("cy", f, ni)])
                    else:
                        sc.op("dve", lambda e, o=ub[j][:, 2 + n0:2 + n0 + N], i=ps[:, b, :N],
                              c_=cy[:, f, n0:n0 + N]: e.tensor_tensor(out=o, in0=i, in1=c_, op=ALU.mult),
                              reads=[("ps", b), ("cy", f, ni)], writes=[("ub", j, ni)])
                    done[oc] = done.get(oc, 0) + 1
                    if oc >= 16 and done[oc] == nn:
                        ubk = [("ub", j, n_) for n_ in range(nn)]
                        sc.op("pool", lambda e, o=ub[j][:, 0:2], i=ust3l[l][:, f, :]: e.tensor_copy(out=o, in_=i),
                              reads=[("ust", l, f)], writes=[("ubs", j)])
                        cwi = lambda tap: convw_sb[:, (l * 3 + tap) * 8 + f:(l * 3 + tap) * 8 + f + 1]
                        sc.op("act", lambda e, o=ctmp[j][:, :T], i=ub[j][:, 0:T], s_=cwi(0):
                              e.activation(out=o, in_=i, func=AF.Copy, scale=s_),
                              reads=ubk + [("ubs", j), "convw"], writes=[("ct", j)])
                        for tap in (1, 2):
                            sc.op("dve", lambda e, o=ctmp[j][:, :T], i=ub[j][:, tap:tap + T], s_=cwi(tap):
                                  e.scalar_tensor_tensor(out=o, in0=i, scalar=s_, in1=o, op0=ALU.mult, op1=ALU.add),
                                  reads=ubk + [("ubs", j), ("ct", j), "convw"], writes=[("ct", j)])
                        sc.op("pool" if f % 2 == 0 else "dve",
                              lambda e, o=cy[:, f, :T], i=ctmp[j][:, :T], g=bgate[:, f, :T]:
                              e.tensor_tensor(out=o, in0=i, in1=g, op=ALU.mult),
                              reads=[("ct", j)] + [("bg", f, n_) for n_ in range(nn)],
                              writes=[("cy", f, n_) for n_ in range(nn)])
                        sc.op("pool", lambda e, o=ust3l[l][:, f, :], i=ub[j][:, T:T + 2]: e.tensor_copy(out=o, in_=i),
                              reads=ubk + [("ubs", j)], writes=[("ust", l, f)])
            for sj in range(2):
                slot = next_slab(si)
                si += 1
                sl = slab8[slot]
                for ocl in range(4):
                    oc = sj * 4 + ocl
                    for ni, (n0, N) in enumerate(nts):
                        b = acc()
                        for kc in range(8):
                            mm(ps[:, b, :N], sl[:, kc, ocl * 128:(ocl + 1) * 128], cy[:, kc, n0:n0 + N],
                               kc == 0, kc == 7, reads=[("slab", slot), ("cy", kc, ni)], writes=[("ps", b)])
                        sc.op("act", lambda e, o=mix[:, oc, n0:n0 + N], i=ps[:, b, :N]:
                              e.activation(out=o, in_=i, func=AF.Copy),
                              reads=[("ps", b)], writes=[("mix", oc, ni)])
                        post_stats_add(ps[:, b, :N], [("ps", b)], oc, ni, N)
            mk = lambda c, ni: [("mix", c, ni)]
            post_stats_finish(nts)
            apply_post(l * 4 + 1, mix, mk, h, hk, T, nts)
            mlp(l, sbase + 8, h, hk, mix, "mix", hn, hid, T, nts)

        def kv_phase(T, nts, kcol0, kb0, meta):
            hk = lambda c, ni: [("h", c, ni)]
            barrier()
            norm_stats(h, hk, T, nts)
            apply_pre(16, h, hk, hn, "hn", T, nts)
            si = 48
            for sj in range(2):
                slot = next_slab(si)
                si += 1
                sl = slab8[slot]
                for ocl, ni in order(sj == 0, nts):
                    n0, N = nts[ni]
                    oc = sj * 4 + ocl
                    b = acc()
                    for kc in range(8):
                        mm(ps[:, b, :N], sl[:, kc, ocl * 128:(ocl + 1) * 128], hn[:, kc, n0:n0 + N],
                           kc == 0, kc == 7, reads=[("slab", slot), ("hn", kc, ni)], writes=[("ps", b)])
                    ki = st["kt"]
                    st["kt"] ^= 1
                    sc.op("act" if ni == 0 else "dve",
                          (lambda e, o=ktmp[ki][:, :N], i=ps[:, b, :N]: e.activation(out=o, in_=i, func=AF.Copy))
                          if ni == 0 else
                          (lambda e, o=ktmp[ki][:, :N], i=ps[:, b, :N]: e.tensor_copy(out=o, in_=i)),
                          reads=[("ps", b)], writes=[("kt", ki)])
                    key = ("kTd", oc, kcol0 + n0)
                    sc.dma("pool", "kvst", kTd[oc, :, kcol0 + n0:kcol0 + n0 + N], ktmp[ki][:, :N],
                           reads=[("kt", ki)], writes=[key])
                    kT_keys[oc].append(key)
            for sj in range(2):
                slot = next_slab(si)
                si += 1
                sl = slab8[slot]
                nblk = 1 if meta else T // 128
                for jb in range(nblk):
                    M = NMETA if meta else 128
                    t0 = jb * 128
                    ni = t0 // 512
                    p0 = 112 if meta else 0
                    kb = kb0 + jb
                    b = acc()
                    for kc in range(8):
                        mm(ps[:M, b, :], hn[:, kc, t0:t0 + M], sl[:, kc, :], kc == 0, kc == 7,
                           reads=[("slab", slot), ("hn", kc, ni)], writes=[("ps", b)])
                    vi = st["vt"]
                    st["vt"] ^= 1
                    sc.op("act" if jb % 2 == 0 else "dve",
                          (lambda e, o=vtmp[vi][:M, :], i=ps[:M, b, :]: e.activation(out=o, in_=i, func=AF.Copy))
                          if jb % 2 == 0 else
                          (lambda e, o=vtmp[vi][:M, :], i=ps[:M, b, :]: e.tensor_copy(out=o, in_=i)),
                          reads=[("ps", b)], writes=[("vt", vi)])
                    dst = vd[sj * 4:(sj + 1) * 4, p0:p0 + M, kb, :].rearrange("c p f -> p c f")
                    srcv = vtmp[vi][:M, :].rearrange("p (c f) -> p c f", c=4)
                    keys = [("vd", sj * 4 + cc, kb) for cc in range(4)]
                    sc.dma("pool", "kvst", dst, srcv, reads=[("vt", vi)], writes=keys)
                    for cc in range(4):
                        v_keys[sj * 4 + cc].append(keys[cc])

        def kv_load(S, c, which):
            nkb = 8 * S + 9
            h0 = nkb // 2
            a_, bnd = (h0, nkb) if which == "hi" else (0, h0)
            sc.dma("pool", "kvld", kTs[:, a_ * 128:bnd * 128], kTd[c, :, a_ * 128:bnd * 128],
                   reads=list(kT_keys[c]), writes=[("kTs", which)])
            sc.dma("pool", "kvld", vs[:, a_:bnd, :], vd[c, :, a_:bnd, :],
                   reads=list(v_keys[c]), writes=[("vs", which)])

        def attention(S):
            nkb = 8 * S + 9
            h0 = nkb // 2
            tiles = list(range(nkb - 1, -1, -1))
            n = nkb
            seq = [(c, t) for c in range(8) for t in range(n)]
            G = len(seq)

            def half(kb):
                return "hi" if kb >= h0 else "lo"

            def issue_load(c, which):
                kv_load(S, c, which)

            def cs_of(t):
                rel = tiles[t] - (8 * S + 1)
                if rel < 0:
                    return 0, rel
                return 128 * (rel // 2), rel

            def S1(g):
                c, t = seq[g]
                kb = tiles[t]
                cs, rel = cs_of(t)
                for hh in range(2):
                    pr = slice(64 * hh, 64 * hh + 64)
                    mm(ps[:, hh, cs:512], kTs[pr, kb * 128:(kb + 1) * 128], qT[pr, c, cs:512], True, rel < 0,
                       reads=[("kTs", half(kb)), ("qT", c)], writes=[("ps", hh)])
                if rel >= 0:
                    for hh in range(2):
                        mm(ps[:, hh, cs:512], ident_sb[:, :], dmask3[:, rel, cs:512], False, True,
                           reads=["ident", "dmask"], writes=[("ps", hh)])

            def S2(g):
                c, t = seq[g]
                cs, rel = cs_of(t)
                b = g % 2
                sc.op("act", lambda e, o=e2[b][:, :, cs:512], i=ps[:, 0:2, cs:512]:
                      e.activation(out=o, in_=i, func=AF.Exp),
                      reads=[("ps", 0), ("ps", 1)], writes=[("e2", b)])

            def S3(g):
                c, t = seq[g]
                cs, rel = cs_of(t)
                b = g % 2
                sc.op("act", lambda e, o=sp2[b][:, :, cs:512], i=e2[b][:, :, cs:512]:
                      e.activation(out=o, in_=i, func=AF.Ln, bias=one_ap, scale=1.0),
                      reads=[("e2", b), "cst1"], writes=[("sp2", b)])

            def S4(g):
                c, t = seq[g]
                kb = tiles[t]
                cs, rel = cs_of(t)
                b = g % 2
                for hh in range(2):
                    pr = slice(64 * hh, 64 * hh + 64)
                    mm(ps[:, 2 + hh, cs:512], kTs[pr, kb * 128:(kb + 1) * 128], qT[pr, c, cs:512], True, False,
                       reads=[("kTs", half(kb)), ("qT", c)], writes=[("ps", 2 + hh)])
                for hh in range(2):
                    mm(ps[:, 2 + hh, cs:512], negT_sb[:, :], sp2[b][:, hh, cs:512], False, t == 0 and rel < 0,
                       reads=[("sp2", b), "negT"], writes=[("ps", 2 + hh)])
                if t > 0:
                    for hh in range(2):
                        mm(ps[:, 2 + hh, cs:512], negones_sb[:, :], Lsum[:, hh, cs:512], False, rel < 0,
                           reads=[("Lsum",), "negones"], writes=[("ps", 2 + hh)])
                if rel >= 0:
                    for hh in range(2):
                        mm(ps[:, 2 + hh, cs:512], ident_sb[:, :], dmask3[:, rel, cs:512], False, True,
                           reads=["ident", "dmask"], writes=[("ps", 2 + hh)])

            def UPD(g):
                c, t = seq[g]
                cs, rel = cs_of(t)
                b = g % 2
                if t == n - 1:
                    return
                if t == 0:
                    sc.op("pool", lambda e, o=Lsum[:, :, :]: e.memset(o, 0.0), writes=[("Lsum",)])
                sc.op("dve", lambda e, o=Lsum[:, :, cs:512], i=sp2[b][:, :, cs:512]:
                      e.tensor_tensor(out=o, in0=o, in1=i, op=ALU.add),
                      reads=[("sp2", b), ("Lsum",)], writes=[("Lsum",)])

            def S6(g):
                c, t = seq[g]
                cs, rel = cs_of(t)
                b = g % 2
                sc.op("act", lambda e, o=A2[b][:, :, cs:512], i=ps[:, 2:4, cs:512]:
                      e.activation(out=o, in_=i, func=AF.Exp),
                      reads=[("ps", 2), ("ps", 3)], writes=[("A2", b)])

            def S7(g):
                c, t = seq[g]
                kb = tiles[t]
                cs, rel = cs_of(t)
                b = g % 2
                Ob = 4 + 2 * (c % 2)
                if t == 0:
                    for hh in range(2):
                        mm(ps[:, Ob + hh, :], ztile[:, :], dmask3[:, 0, :], True, False,
                           reads=["ztile", "dmask"], writes=[("ps", Ob + hh)])
                for hh in range(2):
                    mm(ps[:, Ob + hh, cs:512], vs[:, kb, :], A2[b][:, hh, cs:512], False, t == n - 1,
                       reads=[("vs", half(kb)), ("A2", b)], writes=[("ps", Ob + hh)])
                if kb == h0 and c + 1 < 8:
                    issue_load(c + 1, "hi")
                if t == n - 1:
                    if c + 1 < 8:
                        issue_load(c + 1, "lo")
                    for hh in range(2):
                        pr = slice(64 * hh, 64 * hh + 64)
                        sc.op("dve", lambda e, o=oT[pr, c, :], i=ps[pr, Ob + hh, :]: e.tensor_copy(out=o, in_=i),
                              reads=[("ps", Ob + hh)], writes=[("oT", c, hh)])

            S1(0)
            S2(0)
            if G > 1:
                S1(1)
            for g in range(G + 1):
                if g + 1 < G:
                    S2(g + 1)
                if g < G:
                    S3(g)
                if g >= 1:
                    S6(g - 1)
                if g + 2 < G:
                    S1(g + 2)
                if g < G:
                    S4(g)
                    UPD(g)
                if g >= 1:
                    S7(g - 1)

        def attn_layer(S, lb):
            l = 2 + lb
            sbase = 52 + 20 * lb
            T = 512
            nts = [(0, 512)]
            hok = lambda c, ni: [("ho", c)]
            if lb != 0:
                barrier()
            kv_load(S, 0, "hi")
            kv_load(S, 0, "lo")
            cast_pre(hown, hok, hnB, "hn", T, nts)
            si = sbase
            for sj in range(2):
                slot = next_slab(si)
                si += 1
                sl = slab8[slot]
                for ocl in range(4):
                    oc = sj * 4 + ocl
                    b = acc()
                    for kc in range(8):
                        mm(ps[:, b, :], sl[:, kc, ocl * 128:(ocl + 1) * 128], hnB[:, kc, :], kc == 0, kc == 7,
                           reads=[("slab", slot), ("hn", kc, 0)], writes=[("ps", b)])
                    sc.op("dve", lambda e, o=qT[:, oc, :], i=ps[:, b, :]:
                          e.scalar_tensor_tensor(out=o, in0=i, scalar=0.125, in1=rstd[:, 0:512],
                                                 op0=ALU.mult, op1=ALU.mult),
                          reads=[("ps", b), ("rstd", 0)], writes=[("qT", oc)])
            attention(S)
            for sj in range(2):
                slot = next_slab(si)
                si += 1
                sl = slab8[slot]
                for ocl in range(4):
                    oc = sj * 4 + ocl
                    b = acc()
                    for kc in range(8):
                        mm(ps[:, b, :], sl[:, kc, ocl * 128:(ocl + 1) * 128], oT[:, kc, :], kc == 0, kc == 7,
                           reads=[("slab", slot), ("oT", kc, 0), ("oT", kc, 1)], writes=[("ps", b)])
                    sc.op("act", lambda e, o=mixB[:, oc, :], i=ps[:, b, :]:
                          e.activation(out=o, in_=i, func=AF.Copy),
                          reads=[("ps", b)], writes=[("mixB", oc, 0)])
                    post_stats_add(ps[:, b, :], [("ps", b)], oc, 0, 512)
            mk = lambda c, ni: [("mixB", c, 0)]
            post_stats_finish(nts)
            apply_post(l * 4 + 1, mixB, mk, hown, hok, T, nts)
            mlp(l, sbase + 4, hown, hok, mixB, "mixB", hnB, hidB, T, nts)

        def select_own():
            for c in range(8):
                hv = h[:, c, :].rearrange("p (i two q) -> p i two q", two=2, q=128)
                ev = hv[:, :, 0, :]
                od = hv[:, :, 1, :]
                dst = hown[:, c, :].rearrange("p (i q) -> p i q", q=128)
                j = st["tc"]
                st["tc"] ^= 1
                tmp = tmpc[j][:, 0:512].rearrange("p (i q) -> p i q", q=128)
                sc.op("dve", lambda e, o=tmp, i=ev: e.tensor_scalar(out=o, in0=i, scalar1=pid_sb[:, 0:1], scalar2=None,
                                                                op0=ALU.mult),
                      reads=[("h", c, 0), ("h", c, 1), "pid"], writes=[("tc", j)])
                sc.op("dve", lambda e, o=dst, i=od, t_=tmp:
                      e.scalar_tensor_tensor(out=o, in0=i, scalar=pid_sb[:, 1:2], in1=t_, op0=ALU.mult, op1=ALU.add),
                      reads=[("h", c, 0), ("h", c, 1), ("tc", j), "pid"], writes=[("ho", c)])

        sc.dma("sp", "x", h[:, :, 0:NMETA], metaT.rearrange("(c p) t -> p c t", p=128),
               writes=[("h", c, 0) for c in range(8)])
        mnts = [(0, NMETA)]
        conv_layer(0, NMETA, mnts)
        conv_layer(1, NMETA, mnts)
        kv_phase(NMETA, mnts, 112, 0, True)
        nts = [(0, 512), (512, 512)]
        for S in range(nsg):
            barrier()
            for ni in range(2):
                sc.dma("sp", "x", h[:, :, ni * 512:(ni + 1) * 512],
                       xT[:, S * 1024 + ni * 512:S * 1024 + (ni + 1) * 512].rearrange("(c p) t -> p c t", p=128),
                       writes=[("h", c, ni) for c in range(8)])
            conv_layer(0, 1024, nts, skip_barrier=True)
            conv_layer(1, 1024, nts)
            kv_phase(1024, nts, 128 + S * 1024, 1 + 8 * S, False)
            barrier()
            select_own()
            barrier()
            attn_layer(S, 0)
            attn_layer(S, 1)
            sc.dma("pool", "out", outT[:, S * 512:(S + 1) * 512].rearrange("(c p) t -> p c t", p=128), hown[:, :, :],
                   reads=[("ho", c) for c in range(8)], writes=[("out", S)])
        barrier()
        assert st["slab_k"] == len(slab_seq), (st["slab_k"], len(slab_seq))

        with nc.Block() as block:
            @block.tensor
            def _(e):
                for f in sc.ops["pe"]:
                    f(e)

            @block.scalar
            def _(e):
                for f in sc.ops["act"]:
                    f(e)

            @block.vector
            def _(e):
                for f in sc.ops["dve"]:
                    f(e)

            @block.gpsimd
            def _(e):
                for f in sc.ops["pool"]:
                    f(e)

            @block.sync
            def _(e):
                for f in sc.ops["sp"]:
                    f(e)
    return nc


def _host_consts(p):
    bf = ml_dtypes.bfloat16
    j = np.arange(128)[:, None]
    s = np.arange(128)[None, :]
    negT = np.where(j >= s, -1.0, 0.0).astype(bf)
    negones = np.full((128, 128), -1.0, dtype=bf)
    ones = np.ones((128, 128), dtype=bf)
    dm = np.zeros((128, 8, 4, 128), dtype=np.float32)
    tri = (np.arange(128)[:, None] < np.arange(128)[None, :]).astype(np.float32)
    for rel in range(8):
        for i in range(4):
            qb = 2 * i + p
            if rel < qb:
                dm[:, rel, i, :] = 1.0
            elif rel == qb:
                dm[:, rel, i, :] = tri
    dmask = ((dm - 1.0) * 30000.0).reshape(128, 8 * 512).astype(bf)
    ident = np.eye(128, dtype=np.float32).astype(bf)
    pid = np.zeros((128, 2), dtype=np.float32)
    pid[:, 0] = 1.0 - p
    pid[:, 1] = float(p)
    return negT, negones, ones, dmask, pid, ident


def kernel(x, meta_tokens, norm_gains, conv_in_proj, conv_w, conv_out_proj, kv_norm,
           w_k, w_v, w_q, w_o, mlp_w1, mlp_w2):
    f32 = np.float32
    x = np.asarray(x, dtype=f32)
    B, SEQ, _ = x.shape
    nsg = SEQ // 1024
    assert B * 2 == 8 and SEQ == nsg * 1024
    nc = build(nsg)

    def pc(v):
        return np.asarray(v, dtype=f32).reshape(8, 128).T

    ng = np.asarray(norm_gains, dtype=f32)
    gl = [pc(ng[l, j]) for l in range(4) for j in range(4)] + [pc(np.asarray(kv_norm, dtype=f32))]
    gains = np.ascontiguousarray(np.concatenate(gl, axis=1))
    cw = np.asarray(conv_w, dtype=f32)
    convw = np.ascontiguousarray(np.concatenate([pc(cw[l, t]) for l in range(2) for t in range(3)], axis=1))
    metaT = np.ascontiguousarray(np.asarray(meta_tokens, dtype=f32).T)
    shared = {
        "metaT": metaT, "gains": gains, "convw": convw,
        "conv_in_proj": np.ascontiguousarray(conv_in_proj, dtype=f32),
        "conv_out_proj": np.ascontiguousarray(conv_out_proj, dtype=f32),
        "w_k": np.ascontiguousarray(w_k, dtype=f32), "w_v": np.ascontiguousarray(w_v, dtype=f32),
        "w_q": np.ascontiguousarray(w_q, dtype=f32), "w_o": np.ascontiguousarray(w_o, dtype=f32),
        "mlp_w1": np.ascontiguousarray(mlp_w1, dtype=f32), "mlp_w2": np.ascontiguousarray(mlp_w2, dtype=f32),
    }
    consts = [_host_consts(p) for p in range(2)]
    xTs = [np.ascontiguousarray(x[b].T) for b in range(B)]
    in_maps = []
    for core in range(8):
        b, p = core // 2, core % 2
        negT, negones, ones, dmask, pid, ident = consts[p]
        m = dict(shared)
        m.update({"xT": xTs[b], "pid": pid, "dmask": dmask, "negT": negT, "negones": negones, "ones": ones,
                  "ident": ident})
        in_maps.append(m)
    res = run_bass_kernel_spmd(nc, in_maps, core_ids=list(range(8)))
    out = np.empty((B, SEQ, D), dtype=f32)
    for core in range(8):
        b, p = core // 2, core % 2
        oT = np.asarray(res.results[core]["outT"], dtype=f32)
        o = oT.T.reshape(nsg, 4, 128, D)
        ov = out[b].reshape(nsg, 4, 2, 128, D)
        ov[:, :, p] = o
    return out
```
